# Optimizing a Trainium2 kernel written in Bass

```python
import jax, jax.numpy as jnp
from jax import lax
import numpy as np

D_MODEL = 1024
BATCH = 2
SEQ = 16384
DEPTH = 1
DEC_BATCH = 32
DEC_SEQ = 32
PAST_LEN = 2048

CHUNK = 64
P_DIM = 256
N_HEADS_A = 8
HEAD_DK = 128
HEAD_DV = 128
KEY_DIM = N_HEADS_A * HEAD_DK
VAL_DIM = N_HEADS_A * HEAD_DV
QKV_DIM = 2 * KEY_DIM + VAL_DIM
CONV_A = 4
WIDTH_B = D_MODEL
CONV_B = 3
ALPHA = (2 * DEPTH) ** 0.25
BETA_INIT = (8 * DEPTH) ** -0.25
LN_EPS = 1e-5
RMS_EPS = 1e-6
L2_EPS = 1e-6

OFF_ZA = QKV_DIM
OFF_BETA = OFF_ZA + VAL_DIM
OFF_DECAY = OFF_BETA + N_HEADS_A
OFF_BB = OFF_DECAY + N_HEADS_A
OFF_CB = OFF_BB + WIDTH_B
OFF_UB = OFF_CB + WIDTH_B
OFF_ZB = OFF_UB + WIDTH_B
OFF_GATE = OFF_ZB + WIDTH_B
IN_DIM = OFF_GATE + 2 * D_MODEL

kernel_name = 'hybrid_gdn_shortconv_stream'


def layer_norm(x, g, b):
    xf = x.astype(jnp.float32)
    mu = jnp.mean(xf, -1, keepdims=True)
    var = jnp.mean(jnp.square(xf - mu), -1, keepdims=True)
    return ((xf - mu) * lax.rsqrt(var + LN_EPS) * g.astype(jnp.float32) + b.astype(jnp.float32)).astype(x.dtype)


def l2norm(x):
    return x * lax.rsqrt(jnp.sum(x * x, -1, keepdims=True) + L2_EPS)


def causal_dwconv(u, buf, w):
    k_w = w.shape[0]
    t = u.shape[1]
    full = jnp.concatenate([buf.astype(u.dtype), u], axis=1)
    y = full[:, 0:t] * w[0]
    for j in range(1, k_w):
        y = y + full[:, j:j + t] * w[j]
    return y, full[:, t:]


def gated_delta_chunked(q, k, v, beta, g, s0, chunk):
    bsz, t, h, dk = q.shape
    dv = v.shape[-1]
    n = t // chunk

    def blk(a):
        a = a.reshape((bsz, n, chunk, h) + a.shape[3:])
        return jnp.moveaxis(a, 3, 1)

    q, k, v, beta, g = blk(q), blk(k), blk(v), blk(beta), blk(g)
    gc = jnp.cumsum(g, axis=-1)
    causal = jnp.tril(jnp.ones((chunk, chunk), bool))
    strict = jnp.tril(jnp.ones((chunk, chunk), bool), -1)
    diff = gc[..., :, None] - gc[..., None, :]
    decay = jnp.where(causal, jnp.exp(jnp.where(causal, diff, 0.0)), 0.0)
    kk = jnp.einsum('bhnid,bhnjd->bhnij', k, k)
    a_mat = jnp.where(strict, beta[..., :, None] * kk * decay, 0.0)
    eye = jnp.eye(chunk, dtype=jnp.float32)
    t_mat = lax.linalg.triangular_solve(eye + a_mat, jnp.broadcast_to(eye, a_mat.shape),
                                        left_side=True, lower=True, unit_diagonal=True)
    u_blk = jnp.einsum('bhnij,bhnje->bhnie', t_mat, v * beta[..., None])
    w_blk = jnp.einsum('bhnij,bhnjd->bhnid', t_mat, k * (beta * jnp.exp(gc))[..., None])
    qk = jnp.where(causal, jnp.einsum('bhnid,bhnjd->bhnij', q, k) * decay, 0.0)
    q_dec = q * jnp.exp(gc)[..., None]
    k_dec = k * jnp.exp(gc[..., -1:] - gc)[..., None]
    g_last = jnp.exp(gc[..., -1])

    def step(s, xs):
        u_c, w_c, qk_c, qd_c, kd_c, gl_c = xs
        v_new = u_c - jnp.einsum('bhid,bhde->bhie', w_c, s)
        o_c = jnp.einsum('bhid,bhde->bhie', qd_c, s) + jnp.einsum('bhij,bhje->bhie', qk_c, v_new)
        s = s * gl_c[..., None, None] + jnp.einsum('bhid,bhie->bhde', kd_c, v_new)
        return s, o_c

    xs = tuple(jnp.moveaxis(a, 2, 0) for a in (u_blk, w_blk, qk, q_dec, k_dec, g_last))
    s_fin, o = lax.scan(step, s0, xs)
    o = jnp.moveaxis(o, 0, 2)
    o = jnp.moveaxis(o, 1, 3).reshape(bsz, t, h, dv)
    return o, s_fin


def trunk_layer(h, p, conv_a_buf, s_gdn, conv_b_buf, chunk, w_in, w_conv_a, a_log, dt_bias,
                norm_a_g, w_conv_b, w_proj_a, w_proj_b, w_out, ln1_g, ln1_b, w_ple, w_ple_gate,
                ln2_g, ln2_b):
    f32 = jnp.float32
    bsz, t = h.shape[0], h.shape[1]
    z = h @ w_in
    qkv, conv_a_new = causal_dwconv(z[..., :QKV_DIM], conv_a_buf, w_conv_a)
    qkv = jax.nn.silu(qkv).astype(f32)
    q = l2norm(qkv[..., :KEY_DIM].reshape(bsz, t, N_HEADS_A, HEAD_DK)) * (HEAD_DK ** -0.5)
    k = l2norm(qkv[..., KEY_DIM:2 * KEY_DIM].reshape(bsz, t, N_HEADS_A, HEAD_DK))
    v = qkv[..., 2 * KEY_DIM:].reshape(bsz, t, N_HEADS_A, HEAD_DV)
    beta = jax.nn.sigmoid(z[..., OFF_BETA:OFF_DECAY].astype(f32))
    g = -jnp.exp(a_log.astype(f32)) * jax.nn.softplus(z[..., OFF_DECAY:OFF_BB].astype(f32) + dt_bias.astype(f32))
    o, s_new = gated_delta_chunked(q, k, v, beta, g, s_gdn.astype(f32), chunk)
    gate_a = z[..., OFF_ZA:OFF_BETA].astype(f32).reshape(bsz, t, N_HEADS_A, HEAD_DV)
    o = o * lax.rsqrt(jnp.mean(o * o, -1, keepdims=True) + RMS_EPS) * norm_a_g.astype(f32) * jax.nn.silu(gate_a)
    y_a = o.reshape(bsz, t, VAL_DIM).astype(h.dtype)
    cu = z[..., OFF_CB:OFF_UB] * z[..., OFF_UB:OFF_ZB]
    conv_b_out, conv_b_new = causal_dwconv(cu, conv_b_buf, w_conv_b)
    y_b = z[..., OFF_BB:OFF_CB] * conv_b_out * jax.nn.silu(z[..., OFF_ZB:OFF_GATE])
    gates = jax.nn.sigmoid(z[..., OFF_GATE:])
    merged = gates[..., :D_MODEL] * (y_a @ w_proj_a) + gates[..., D_MODEL:] * (y_b @ w_proj_b)
    h1 = layer_norm(ALPHA * h + merged @ w_out, ln1_g, ln1_b)
    ple = jax.nn.sigmoid(h1 @ w_ple_gate) * (p @ w_ple)
    h2 = layer_norm(ALPHA * h1 + ple, ln2_g, ln2_b)
    return h2, conv_a_new, s_new.astype(s_gdn.dtype), conv_b_new


def setup_inputs(seed: int = 0) -> dict:
    key = jax.random.key(seed)
    ks = jax.random.split(key, 32)
    nrm = jax.random.normal
    d = D_MODEL
    dt = jnp.exp(jax.random.uniform(ks[10], (DEPTH, N_HEADS_A), minval=np.log(1e-3), maxval=np.log(1e-1)))
    return {
        'x_prompt': nrm(ks[0], (BATCH, SEQ, d), jnp.float32),
        'x_sample': nrm(ks[1], (DEC_BATCH, DEC_SEQ, d), jnp.float32),
        'state_conv_a': nrm(ks[2], (DEPTH, DEC_BATCH, CONV_A - 1, QKV_DIM), jnp.float32),
        'state_gdn': 0.1 * nrm(ks[3], (DEPTH, DEC_BATCH, N_HEADS_A, HEAD_DK, HEAD_DV), jnp.float32),
        'state_conv_b': nrm(ks[4], (DEPTH, DEC_BATCH, CONV_B - 1, WIDTH_B), jnp.float32),
        'p_prompt': nrm(ks[5], (DEPTH, BATCH, SEQ, P_DIM), jnp.float32),
        'p_sample': nrm(ks[6], (DEPTH, DEC_BATCH, DEC_SEQ, P_DIM), jnp.float32),
        'ln_in_g': 1.0 + 0.01 * nrm(ks[7], (d,), jnp.float32),
        'ln_in_b': 0.01 * nrm(ks[8], (d,), jnp.float32),
        'w_in': nrm(ks[9], (DEPTH, d, IN_DIM), jnp.float32) * d ** -0.5,
        'w_conv_a': nrm(ks[11], (DEPTH, CONV_A, QKV_DIM), jnp.float32) * CONV_A ** -0.5,
        'a_log': jnp.log(jax.random.uniform(ks[12], (DEPTH, N_HEADS_A), minval=1.0, maxval=16.0)),
        'dt_bias': dt + jnp.log(-jnp.expm1(-dt)),
        'norm_a_g': 1.0 + 0.01 * nrm(ks[13], (DEPTH, HEAD_DV), jnp.float32),
        'w_conv_b': nrm(ks[14], (DEPTH, CONV_B, WIDTH_B), jnp.float32) * CONV_B ** -0.5,
        'w_proj_a': nrm(ks[15], (DEPTH, VAL_DIM, d), jnp.float32) * VAL_DIM ** -0.5 * BETA_INIT,
        'w_proj_b': nrm(ks[16], (DEPTH, WIDTH_B, d), jnp.float32) * WIDTH_B ** -0.5 * BETA_INIT,
        'w_out': nrm(ks[17], (DEPTH, d, d), jnp.float32) * d ** -0.5 * BETA_INIT,
        'ln1_g': 1.0 + 0.01 * nrm(ks[18], (DEPTH, d), jnp.float32),
        'ln1_b': 0.01 * nrm(ks[19], (DEPTH, d), jnp.float32),
        'w_ple': nrm(ks[20], (DEPTH, P_DIM, d), jnp.float32) * P_DIM ** -0.5 * BETA_INIT,
        'w_ple_gate': nrm(ks[21], (DEPTH, d, d), jnp.float32) * d ** -0.5,
        'ln2_g': 1.0 + 0.01 * nrm(ks[22], (DEPTH, d), jnp.float32),
        'ln2_b': 0.01 * nrm(ks[23], (DEPTH, d), jnp.float32),
    }


def reference(x_prompt, x_sample, state_conv_a, state_gdn, state_conv_b, p_prompt, p_sample,
              ln_in_g, ln_in_b, w_in, w_conv_a, a_log, dt_bias, norm_a_g, w_conv_b, w_proj_a,
              w_proj_b, w_out, ln1_g, ln1_b, w_ple, w_ple_gate, ln2_g, ln2_b):
    def encode(x, p, conv_a, s_gdn, conv_b, chunk):
        h = layer_norm(x, ln_in_g, ln_in_b)
        new_a, new_s, new_b = [], [], []
        for i in range(DEPTH):
            h, ca, s, cb = trunk_layer(h, p[i], conv_a[i], s_gdn[i], conv_b[i], chunk, w_in[i],
                                       w_conv_a[i], a_log[i], dt_bias[i], norm_a_g[i], w_conv_b[i],
                                       w_proj_a[i], w_proj_b[i], w_out[i], ln1_g[i], ln1_b[i],
                                       w_ple[i], w_ple_gate[i], ln2_g[i], ln2_b[i])
            new_a.append(ca)
            new_s.append(s)
            new_b.append(cb)
        return h, jnp.stack(new_a), jnp.stack(new_s), jnp.stack(new_b)

    bp = x_prompt.shape[0]
    zero_a = jnp.zeros((DEPTH, bp, CONV_A - 1, QKV_DIM), x_prompt.dtype)
    zero_s = jnp.zeros((DEPTH, bp, N_HEADS_A, HEAD_DK, HEAD_DV), state_gdn.dtype)
    zero_b = jnp.zeros((DEPTH, bp, CONV_B - 1, WIDTH_B), x_prompt.dtype)
    y_prompt, ca_p, s_p, cb_p = encode(x_prompt, p_prompt, zero_a, zero_s, zero_b, CHUNK)
    y_sample, ca_s, s_s, cb_s = encode(x_sample, p_sample, state_conv_a, state_gdn, state_conv_b,
                                       x_sample.shape[1])
    return (y_prompt, y_sample, ca_p, s_p, cb_p, ca_s, s_s, cb_s)
```

```python
import numpy as np
from contextlib import ExitStack
import concourse.bass as bass
import concourse.mybir as mybir
from concourse.bass_utils import run_bass_kernel_spmd

F32 = mybir.dt.float32
BF16 = mybir.dt.bfloat16
F32R = mybir.dt.float32r
AF = mybir.ActivationFunctionType
ALU = mybir.AluOpType

D = 1024
KC = 8
NH = 8
QKV = 3072
OFF_ZA = 3072
OFF_BETA = 4096
OFF_DEC = 4104
OFF_BB = 4112
W1C = 1028
ALPHA = 2.0 ** 0.25
LN_EPS = 1e-5
RMS_EPS = 1e-6
L2_EPS = 1e-6
NEG = -30000.0
USE_R32 = False


class Op:
    __slots__ = ("eng", "fn", "deps", "inc", "val", "dkey", "sem")

    def __init__(self, eng, fn, dkey):
        self.eng = eng
        self.fn = fn
        self.deps = set()
        self.inc = False
        self.val = 0
        self.dkey = dkey
        self.sem = None


class Sched:
    ENGS = ("pe", "act", "dve", "pool", "sp")

    def __init__(self):
        self.q = {e: [] for e in self.ENGS}
        self.lastw = {}
        self.readers = {}
        self.fence = {e: set() for e in self.ENGS}
        self.dmas = []
        self.all_dmas = []

    def add(self, eng, fn, r=(), w=(), x=(), dkey=None):
        op = Op(eng, fn, dkey)
        deps = set()
        for k in r:
            o = self.lastw.get(k)
            if o is not None:
                deps.add(o)
        for k in w:
            o = self.lastw.get(k)
            if o is not None:
                deps.add(o)
            deps.update(self.readers.get(k, ()))
        for k in x:
            o = self.lastw.get(k)
            if o is not None and (o.eng != eng or o.dkey is not None):
                deps.add(o)
        for k in r:
            self.readers.setdefault(k, []).append(op)
        for k in w:
            self.lastw[k] = op
            self.readers[k] = []
        for k in x:
            self.lastw[k] = op
            self.readers[k] = []
        deps.update(self.fence[eng])
        self.fence[eng] = set()
        deps.discard(op)
        if eng == "pe":
            deps = {d for d in deps if not (d.eng == "pe" and d.dkey is None)}
        op.deps = deps
        for d in deps:
            d.inc = True
        self.q[eng].append(op)
        if dkey is not None:
            op.inc = True
            self.dmas.append(op)
            self.all_dmas.append(op)
        return op

    def barrier(self):
        f = set(self.dmas)
        for e in self.ENGS:
            if self.q[e]:
                f.add(self.q[e][-1])
        for e in self.ENGS:
            self.fence[e] = set(f)
        self.dmas = []

    def emit(self, nc, es):
        sem_eng = {e: es.enter_context(nc.semaphore("s_" + e)) for e in self.ENGS}
        dsem = {}
        dcnt = {}
        for e in self.ENGS:
            c = 0
            for op in self.q[e]:
                if op.dkey is not None:
                    if op.dkey not in dsem:
                        dsem[op.dkey] = es.enter_context(nc.semaphore("d%d" % len(dsem)))
                        dcnt[op.dkey] = 0
                    inc = 1 if op.dkey == "cc" else 16
                    dcnt[op.dkey] += inc
                    op.sem = dsem[op.dkey]
                    op.val = dcnt[op.dkey]
                else:
                    op.sem = sem_eng[e]
                    if op.inc:
                        c += 1
                    op.val = c if op.inc else None
        assert len(dsem) < 200, len(dsem)
        block = es.enter_context(nc.Block())

        def run(e):
            def body(eng):
                waited = {}
                for op in self.q[e]:
                    need = {}
                    for d in op.deps:
                        assert d.val is not None
                        if waited.get(d.sem, 0) < d.val:
                            if need.get(d.sem, (None, 0))[1] < d.val:
                                need[d.sem] = (d.sem, d.val)
                    for s, v in need.values():
                        eng.wait_ge(s, v)
                        waited[s] = v
                    ins = op.fn(eng)
                    if op.dkey is not None:
                        if op.dkey == "cc":
                            ins.then_inc(op.sem)
                        else:
                            ins.then_inc(op.sem, 16)
                    elif op.inc:
                        ins.then_inc(op.sem, 1)
            return body

        block.tensor(run("pe"))
        block.scalar(run("act"))
        block.vector(run("dve"))
        block.gpsimd(run("pool"))
        block.sync(run("sp"))


class Tl:
    __slots__ = ("ap", "k", "x", "r", "slot")

    def __init__(self, ap, k, x=None, r=False):
        self.ap = ap
        self.k = k
        self.x = x
        self.r = r

    def __getitem__(self, idx):
        return Tl(self.ap[idx], self.k, self.x, self.r)

    @property
    def o(self):
        return self.ap.bitcast(F32R) if (self.r and USE_R32) else self.ap


def consts_np():
    c = {}
    I = np.eye(128, dtype=np.float32)
    c["ident"] = I
    c["ones"] = np.ones((128, 128), np.float32)
    c["ones128"] = np.full((128, 128), 128.0, np.float32)
    idx = np.arange(128)
    for name, blk in (("p", 128), ("s", 32)):
        same = (idx[:, None] // blk) == (idx[None, :] // blk)
        upper = (idx[:, None] <= idx[None, :]) & same
        c["tri_" + name] = upper.astype(np.float32)
        c["mneg_" + name] = np.where(upper, 0.0, NEG).astype(np.float32)
        c["notdiag_" + name] = (1.0 - I).astype(np.float32)
        last = (idx[None, :] == (idx[:, None] // blk) * blk + blk - 1)
        c["last_" + name] = last.astype(np.float32)
    for s in range(4):
        m = np.zeros((128, 128), np.float32)
        m[:, 32 * s:32 * s + 32] = 1.0
        c["colmask%d" % s] = m
    rm = np.zeros((128, 128), np.float32)
    for s in range(4):
        rm[32 * s:32 * s + 32, s] = 1.0
    c["rowmask"] = rm
    names = list(c.keys())
    arr = np.concatenate([c[n] for n in names], axis=1)
    offs = {n: i * 128 for i, n in enumerate(names)}
    return np.ascontiguousarray(arr), offs


def build(T, NT1=2, NT2=4):
    T4 = T // 4
    assert T % (128 * NT1) == 0 and T4 % (128 * NT2) == 0
    nc = bass.Bass("TRN2", target_bir_lowering=False)
    cnp, coff = consts_np()
    NCON = cnp.shape[1]

    def din(name, shape, dt=F32):
        return nc.dram_tensor(name, list(shape), dt, kind="ExternalInput").ap()

    def dout(name, shape, dt=F32):
        return nc.dram_tensor(name, list(shape), dt, kind="ExternalOutput").ap()

    xp = din("xp", [T, D])
    x2 = din("x2", [T4, D])
    xhalo = din("xhalo", [128, D])
    halo_on = din("halo_on", [128, 1])
    p2 = din("p2", [T4, 256])
    xs = din("xs", [128, D])
    ps = din("ps", [128, 256])
    scaT = din("scaT", [4, 768, 4, 3])
    sgdn = din("sgdn", [4, 8, 128, 128])
    scbT = din("scbT", [D, 4, 2])
    w1all = din("w1all", [4, D, W1C])
    w1own = din("w1own", [D, W1C])
    wcaT_all = din("wcaT_all", [4, 768, 4])
    wcaT_own = din("wcaT_own", [768, 4])
    alb_all = din("alb_all", [4, 2])
    alb_own = din("alb_own", [1, 2])
    dtb_all = din("dtb_all", [4, 2])
    dtb_own = din("dtb_own", [1, 2])
    normg = din("normg", [1, 128])
    w2r = din("w2r", [D, 8 * 768])
    wb = din("wb", [D, D])
    wa_all = din("wa_all", [4, 256, D])
    wa_own = din("wa_own", [256, D])
    wout = din("wout", [D, D])
    wpg = din("wpg", [D, D])
    wple = din("wple", [256, D])
    wcbT = din("wcbT", [D, 3])
    lnp = din("lnp", [6, D])
    cdr = din("consts", [128, NCON])

    y2 = dout("y2", [T4, D])
    cap = dout("cap", [768, 1, 3])
    Sp = dout("Sp", [2, 128, 128])
    cbp = dout("cbp", [D, 1, 2])
    ys = dout("ys", [128, D])
    cas = dout("cas", [4, 768, 4, 3])
    Ss = dout("Ss", [4, 8, 128, 128])
    cbs = dout("cbs", [D, 4, 2])

    pa_loc = nc.dram_tensor("pa_loc", [T, D], F32).ap()
    bfw = {}
    for nm, src in (("w1all", w1all), ("w1own", w1own), ("wa_all", wa_all), ("wa_own", wa_own), ("w2r", w2r),
                    ("wb", wb), ("wout", wout), ("wpg", wpg), ("wple", wple)):
        bfw[nm] = (nc.dram_tensor(nm + "_bf", list(src.shape), BF16).ap(), src)
    pa_red = nc.dram_tensor("pa_red", [T4, D], F32).ap()

    S = Sched()
    es = ExitStack()
    with es:
        PERS = 6 * 1024 + NCON + 1400
        pers = es.enter_context(nc.sbuf_tensor("pers", [128, PERS], F32))
        ARENA = 53000 - PERS
        arena = es.enter_context(nc.sbuf_tensor("arena", [128, ARENA], F32))
        banks = [es.enter_context(nc.psum_tensor("pb%d" % i, [128, 512], F32)) for i in range(8)]
        pstate = {"i": 0, "off": 0, "poff": 0, "n": 0}

        psm_state = {"mode": "rr8", "free": []}

        def psum(dt=F32):
            if psm_state["mode"] == "rr8":
                b = pstate["i"] % 8
            else:
                b = 7
            pstate["i"] += 1
            ap = banks[b][:, :]
            if dt != F32:
                ap = ap.bitcast(dt)
            return Tl(ap, "ps%d" % b, "ps%d" % b)

        def psum_managed_init():
            psm_state["mode"] = "managed"
            psm_state["free"] = [(b, hf) for b in range(7) for hf in range(2)]

        def psum_m():
            assert psm_state["free"], "out of managed PSUM slots"
            b, hf = psm_state["free"].pop(0)
            t = Tl(banks[b][:, hf * 256:(hf + 1) * 256], "ps%d" % b, "ps%d" % b)
            t.slot = (b, hf)
            return t

        def rel(t):
            psm_state["free"].append(t.slot)

        def alloc(words, shape=None, dt=F32, pool="arena", name=None, r32=False):
            key = "off" if pool == "arena" else "poff"
            base = arena if pool == "arena" else pers
            lim = ARENA if pool == "arena" else PERS
            o = pstate[key]
            assert o + words <= lim, (pool, o, words, lim)
            pstate[key] = o + words
            ap = base[:, o:o + words]
            if dt != F32:
                ap = ap.bitcast(dt)
            if shape is not None:
                names = " ".join("a%d" % i for i in range(len(shape)))
                kw = {"a%d" % i: s for i, s in enumerate(shape)}
                ap = ap.rearrange("p (%s) -> p %s" % (names, names), **kw)
            pstate["n"] += 1
            return Tl(ap, name or ("t%d" % pstate["n"]), None, r32)

        def arena_reset():
            pstate["off"] = 0

        def keys(ts):
            ks, xs_ = [], []
            for t in ts:
                if t is None:
                    continue
                if t.x is not None:
                    xs_.append(t.x)
                else:
                    ks.append(t.k)
            return ks, xs_

        def emit(eng, fn, r, w):
            rk, rx = keys(r)
            wk, wx = keys(w)
            return S.add(eng, fn, r=rk, w=wk, x=rx + wx)

        def dma(out, in_, eng="sp"):
            rk, _ = keys([in_])
            wk, _ = keys([out])
            dkey = "dma_" + out.k if not out.k.startswith("dram") else "dma_" + in_.k
            return S.add(eng, lambda e: e.dma_start(out=out.o, in_=(in_.ap.bitcast(F32R) if (out.r and USE_R32) else in_.ap)), r=rk, w=wk, dkey=dkey)

        def act(out, in_, func, bias=None, scale=None, accum=None, eng="act"):
            r = [in_]
            kw = {}
            if isinstance(bias, Tl):
                r.append(bias); kw["bias"] = bias.ap
            elif bias is not None:
                kw["bias"] = bias
            if isinstance(scale, Tl):
                r.append(scale); kw["scale"] = scale.ap
            elif scale is not None:
                kw["scale"] = scale
            w = [out]
            if accum is not None:
                w.append(accum); kw["accum_out"] = accum.ap
            return emit(eng, lambda e: e.activation(out=out.o, in_=in_.ap, func=func, **kw), r, w)

        def tt(out, a, b, op, eng="dve"):
            return emit(eng, lambda e: e.tensor_tensor(out=out.o, in0=a.ap, in1=b.ap, op=op), [a, b], [out])

        def ts(out, a, s1, op0, s2=None, op1=None, eng="dve", accum=None):
            r = [a]
            v1 = s1.ap if isinstance(s1, Tl) else s1
            v2 = s2.ap if isinstance(s2, Tl) else s2
            if isinstance(s1, Tl):
                r.append(s1)
            if isinstance(s2, Tl):
                r.append(s2)
            kw = {}
            w = [out]
            if op1 is not None:
                kw["op1"] = op1
            if accum is not None:
                kw["accum_out"] = accum.ap
                w.append(accum)
            return emit(eng, lambda e: e.tensor_scalar(out=out.o, in0=a.ap, scalar1=v1, scalar2=v2, op0=op0, **kw), r, w)

        def stt(out, a, s, b, op0, op1, eng="dve", accum=None):
            r = [a, b]
            v = s.ap if isinstance(s, Tl) else s
            if isinstance(s, Tl):
                r.append(s)
            kw = {}
            w = [out]
            if accum is not None:
                kw["accum_out"] = accum.ap
                w.append(accum)
            return emit(eng, lambda e: e.scalar_tensor_tensor(out=out.o, in0=a.ap, scalar=v, in1=b.ap, op0=op0, op1=op1, **kw), r, w)

        def copy(out, in_, eng="act"):
            if eng == "act":
                return act(out, in_, AF.Copy)
            return emit(eng, lambda e: e.tensor_copy(out=out.o, in_=in_.ap), [in_], [out])

        def mm(out, lhsT, rhs, start=True, stop=True, r32=False):
            la, ra = lhsT.ap, rhs.ap
            if r32 and USE_R32:
                la = la.bitcast(F32R)
                ra = ra.bitcast(F32R)
            return emit("pe", lambda e: e.matmul(out.ap, la, ra, start=start, stop=stop), [lhsT, rhs], [out])

        def tr(out, in_, ident):
            return emit("pe", lambda e: e.transpose(out.ap, in_.ap, ident.ap), [in_, ident], [out])

        def memset(out, val, eng="pool"):
            return emit(eng, lambda e: e.memset(out.ap, val), [], [out])

        def dr(ap, name):
            return Tl(ap, "dram_" + name)

        def cast_weights(names):
            for nm in names:
                dst, src = bfw[nm]
                if len(dst.shape) == 3:
                    for i in range(dst.shape[0]):
                        dma(dr(dst[i], "%s_bf%d" % (nm, i)), dr(src[i], "%s_src%d" % (nm, i)), eng="pool")
                else:
                    rows = dst.shape[0]
                    step = 512
                    for r0 in range(0, rows, step):
                        r1 = min(rows, r0 + step)
                        dma(dr(dst[r0:r1, :], "%s_bf_r%d" % (nm, r0) if rows > step and nm not in ("w1own", "wa_own") else nm + "_bf"),
                            dr(src[r0:r1, :], "%s_src%d" % (nm, r0)), eng="pool")
        cast_weights(["w1all", "wa_all", "w1own", "wa_own"])
        w1all_b, w1own_b, wa_all_b, wa_own_b = bfw["w1all"][0], bfw["w1own"][0], bfw["wa_all"][0], bfw["wa_own"][0]
        w2r_b, wb_b, wout_b, wpg_b, wple_b = bfw["w2r"][0], bfw["wb"][0], bfw["wout"][0], bfw["wpg"][0], bfw["wple"][0]

        lnb = [alloc(1024, pool="pers", name="lnb%d" % i) for i in range(6)]
        cst = alloc(NCON, pool="pers", name="consts")
        for i in range(6):
            dma(lnb[i], dr(lnp[i, :].partition_broadcast(128), "lnp"))
        dma(cst, dr(cdr, "consts"))

        def C(name):
            o = coff[name]
            return cst[:, o:o + 128]

        ident = C("ident")
        PAS = alloc(1024, pool="pers", name="PAS")
        ngb = alloc(128, pool="pers", name="ngb")
        dma(ngb, dr(normg[0, :].partition_broadcast(128), "normg"))
        ts(ngb, ngb, 0.5, ALU.mult)
        wcb = alloc(24, shape=[8, 3], pool="pers", name="wcb")
        dma(wcb, dr(wcbT.rearrange("(c p) j -> p c j", p=128), "wcbT"))
        hon = alloc(1, pool="pers", name="hon")
        dma(hon, dr(halo_on, "halo_on"))
        eps_t = {}
        for nm, v in (("ln", LN_EPS), ("one", 1.0), ("l2q", 4 * L2_EPS * 128.0), ("l2k", 4 * L2_EPS), ("rms", RMS_EPS), ("ln4", 4 * LN_EPS)):
            t = alloc(1, pool="pers", name="c_" + nm)
            memset(t, v)
            eps_t[nm] = t

        def layer_norm_tile(xt, g_bc, b_bc, scr, eps_key="ln"):
            st = scr[:, 0:12]
            mv = scr[:, 12:14]
            rs = scr[:, 14:15]
            nb = scr[:, 15:16]
            emit("dve", lambda e: e.bn_stats(out=st.ap[:, 0:6], in_=xt.ap[:, 0:512]), [xt], [st])
            emit("dve", lambda e: e.bn_stats(out=st.ap[:, 6:12], in_=xt.ap[:, 512:1024]), [xt], [st])
            emit("dve", lambda e: e.bn_aggr(out=mv.ap, in_=st.ap), [st], [mv])
            act(rs, mv[:, 1:2], AF.Ln, bias=eps_t[eps_key], scale=1.0)
            act(rs, rs, AF.Exp, scale=-0.5)
            stt(nb, mv[:, 0:1], -1.0, rs, ALU.mult, ALU.mult)
            act(xt, xt, AF.Identity, bias=nb, scale=rs)
            tt(xt, xt, g_bc, ALU.mult, eng="dve")
            tt(xt, xt, b_bc, ALU.add, eng="dve")

        def transpose_to_fm(src, dstT, t, ncols_chunks):
            for k0 in range(0, ncols_chunks, 4):
                n = min(4, ncols_chunks - k0)
                pt = psum()
                for k in range(n):
                    tr(pt[:, k * 128:(k + 1) * 128], src[:, (k0 + k) * 128:(k0 + k + 1) * 128], ident)
                o = dstT[:, k0:k0 + n, t * 128:(t + 1) * 128]
                i = Tl(pt.ap[:, 0:n * 128].rearrange("p (k n) -> p k n", k=n), pt.k, pt.x)
                copy(o, i, eng="act")

        psum_managed_init()
        arena_reset()
        NTm = NT1
        Nm = 128 * NTm
        XH = alloc(NTm * 1024, shape=[NTm, 1024], name="XH")
        HTs = [alloc(KC * Nm // 2, shape=[KC, Nm], dt=BF16, name="HT%d" % i) for i in range(2)]
        W1 = alloc(KC * W1C // 2, shape=[KC, W1C], dt=BF16, name="W1")
        wca = alloc(24, shape=[6, 4], name="wca")
        dtbb = alloc(2, name="dtbb")
        nAb = alloc(2, name="nAb")
        scr = alloc(32, name="lnscr")
        ZP = [alloc(Nm + 3, name="zpad%d" % i) for i in range(2)]
        carry = alloc(18, shape=[6, 3], name="carry")
        FMs = [[alloc(Nm, name="fm%d_%d" % (i, c)) for c in range(6)] for i in range(2)]
        cvt = [alloc(Nm, name="cvt%d" % i) for i in range(2)]
        SMs = [alloc(NTm * 4, shape=[NTm, 4], name="smalltok%d" % i) for i in range(2)]
        SB = [[alloc(128, name="S_%d_%d" % (h, s)) for s in range(4)] for h in range(2)]
        SB2 = [[alloc(128, name="S2_%d_%d" % (h, s)) for s in range(4)] for h in range(2)]
        WAP = alloc(1024, shape=[2, 1024], dt=BF16, name="WAP")
        NRING = 2
        ring = {}

        def rt(name, words=128, shape=None, dt=F32, r32=False):
            if name not in ring:
                ring[name] = [[alloc(words, shape=shape, dt=dt, name="%s_%d" % (name, i), r32=r32) for i in range(NRING)], 0]
            lst = ring[name]
            t = lst[0][lst[1] % NRING]
            lst[1] += 1
            return t

        WNAMES = ["qn", "kn", "gbc", "dtp", "DT", "EG", "DTS", "X", "XT", "Y0", "P0", "P1", "PT0", "PT1", "Ya", "Yb", "junk"]
        WSL = []
        for i in range(4):
            d_ = {n: alloc(128, name="w%d_%s" % (i, n)) for n in WNAMES}
            d_["sq"] = alloc(256, name="w%d_sq" % i)
            d_["rn"] = alloc(256, name="w%d_rn" % i)
            WSL.append(d_)
        PCH = []
        for i in range(8):
            d_ = {n: alloc(128, name="p%d_%s" % (i, n)) for n in ("TT", "QKT", "kgT", "qdT", "kd", "vt")}
            d_["sm"] = alloc(8, name="p%d_sm" % i)
            PCH.append(d_)
        PTL = []
        for i in range(4):
            PTL.append(dict(sg=alloc(256, name="pt%d_sg" % i), yat=alloc(256, name="pt%d_yat" % i), gcc=alloc(2, name="pt%d_gcc" % i)))
        RC = dict(r=[alloc(128, name="rc_r%d" % h) for h in range(2)], vn=[alloc(128, name="rc_vn%d" % h) for h in range(2)])
        RCT = [dict(y1=[alloc(128, name="rt%d_y1%d" % (i, h)) for h in range(2)], j2=[alloc(128, name="rt%d_j2%d" % (i, h)) for h in range(2)],
                    ss=alloc(2, name="rt%d_ss" % i), rstd=alloc(2, name="rt%d_rstd" % i),
                    yT=alloc(128, shape=[256], dt=BF16, name="rt%d_yT" % i), pat=alloc(1024, name="rt%d_pat" % i)) for i in range(2)]
        SX = []
        for i in range(2):
            base_names = ["qn", "kn", "gbc", "dtp", "DT", "EG", "DTS", "X", "XT", "Y0", "P0", "P1"]
            SX.append([WSL[2 + i][n] for n in base_names])

        def group_front(xsrc, NT, mode, pr, first_group, last_group, halo_src, ca_out, prep, gi=0):
            HT, FM, SM = HTs[gi % 2], FMs[gi % 2], SMs[gi % 2]
            N = 128 * NT
            nseq = 1 if mode == "p" else 4
            L = N if mode == "p" else 32
            if prep:
                dma(XH[:, 0:NT, :], dr(xsrc.rearrange("(t p) d -> p t d", p=128), "x"))
                for t in range(NT):
                    layer_norm_tile(XH[:, t, :], lnb[0], lnb[1], scr)
                    yield
                    transpose_to_fm(XH[:, t, :], HT, t, KC)
                    yield
            if pr.get("load"):
                dma(W1, dr(pr["w1"].rearrange("(k p) n -> p k n", p=128), pr["w1k"]))
                dma(wca, dr(pr["wcaT"].rearrange("(c p) j -> p c j", p=128), "wcaT"))
                dma(dtbb, dr(pr["dtb"].partition_broadcast(128), "dtb"))
                dma(nAb, dr(pr["alb"].partition_broadcast(128), "alb"))
                dma(WAP, dr(pr["wa"].rearrange("(k p) n -> p k n", p=128), pr["wak"]))
                act(nAb, nAb, AF.Exp)
                ts(nAb, nAb, -1.0, ALU.mult)
            prev_silu = None
            for c in range(6):
                pz = psum()
                for k in range(KC):
                    mm(pz[:, 0:N], W1[:, k, c * 128:(c + 1) * 128], HT[:, k, 0:N], start=(k == 0), stop=(k == KC - 1))
                zp = ZP[c % 2]
                zv = Tl(zp.ap[:, 0:nseq * (L + 3)].rearrange("p (s l) -> p s l", s=nseq), zp.k)
                pv = Tl(pz.ap[:, 0:N].rearrange("p (s l) -> p s l", s=nseq), pz.k, pz.x)
                copy(zv[:, :, 3:3 + L], pv, eng="act")
                if mode == "p":
                    if first_group:
                        memset(zv[:, :, 0:3], 0.0)
                    else:
                        copy(zv[:, 0, 0:3], carry[:, c, :], eng="pool")
                else:
                    dma(zv[:, :, 0:3], dr(halo_src[pr["hp"], c * 128:(c + 1) * 128, :, :], "scaT"))
                if mode == "p":
                    copy(carry[:, c, :], zv[:, 0, L:L + 3], eng="pool")
                    if last_group:
                        dma(dr(ca_out[c * 128:(c + 1) * 128, 0, :], "cap"), carry[:, c, :])
                else:
                    st3 = rt("st3", 12, shape=[4, 3])
                    copy(st3, zv[:, :, L:L + 3], eng="pool")
                    dma(dr(ca_out[pr["hp"], c * 128:(c + 1) * 128, :, :], "cas"), st3)
                yield
                cv = cvt[c % 2]
                cvv = Tl(cv.ap[:, 0:N].rearrange("p (s l) -> p s l", s=nseq), cv.k)
                ts(cvv, zv[:, :, 0:L], wca[:, c, 0:1], ALU.mult, eng="dve")
                for j in (1, 2, 3):
                    stt(cvv, zv[:, :, j:j + L], wca[:, c, j:j + 1], cvv, ALU.mult, ALU.add, eng="dve")
                yield
                if prev_silu is not None:
                    for _ in prev_silu():
                        yield

                def silu_steps(c=c, cv=cv):
                    th = rt("tanh_fm", Nm)
                    act(th[:, 0:N], cv[:, 0:N], AF.Tanh, scale=0.5)
                    yield
                    stt(FM[c][:, 0:N], th[:, 0:N], 1.0, cv[:, 0:N], ALU.add, ALU.mult)
                    yield
                prev_silu = silu_steps
            for _ in prev_silu():
                yield
            psm = psum()
            for t in range(NT):
                for k in range(KC):
                    mm(psm[:, t * 4:(t + 1) * 4], HT[:, k, t * 128:(t + 1) * 128], W1[:, k, 1024:1028], start=(k == 0), stop=(k == KC - 1))
            psv = Tl(psm.ap[:, 0:NT * 4].rearrange("p (t f) -> p t f", t=NT), psm.k, psm.x)
            act(SM[:, 0:NT, 0:2], psv[:, :, 0:2], AF.Tanh, scale=0.5)
            for t in range(NT):
                tt(SM[:, t, 2:4], psv[:, t, 2:4], dtbb, ALU.add)
            yield
            ts(SM[:, 0:NT, 0:2], SM[:, 0:NT, 0:2], 1.0, ALU.add, 0.5, ALU.mult)
            act(SM[:, 0:NT, 2:4], SM[:, 0:NT, 2:4], AF.Exp)
            yield
            act(SM[:, 0:NT, 2:4], SM[:, 0:NT, 2:4], AF.Ln, bias=eps_t["one"], scale=1.0)
            yield
            for t in range(NT):
                tt(SM[:, t, 2:4], SM[:, t, 2:4], nAb, ALU.mult)
            yield

        def chain(t, h, W, P, PT_, mode, gi=0):
            HT, FM, SM = HTs[gi % 2], FMs[gi % 2], SMs[gi % 2]
            sfx = "_p" if mode == "p" else "_s"
            nlev = 7 if mode == "p" else 5
            tsl = slice(t * 128, (t + 1) * 128)
            gcc = PT_["gcc"]
            beta = P["sm"][:, 0:1]
            qr, kr, vr = FM[h][:, tsl], FM[2 + h][:, tsl], FM[4 + h][:, tsl]
            sq, rn = W["sq"], W["rn"]
            copy(beta, SM[:, t, h:h + 1], eng="pool")
            act(sq[:, 0:128], qr, AF.Square)
            act(sq[:, 128:256], kr, AF.Square)
            gbc = W["gbc"]
            ts(gbc, C("ones"), SM[:, t, 2 + h:3 + h], ALU.mult, eng="pool")
            if h == 0:
                pg = psum_m()
                mm(pg[:, 0:2], C("tri" + sfx), SM[:, t, 2:4])
            yield
            pss = psum_m()
            mm(pss[:, 0:128], C("ones128"), sq[:, 0:128])
            mm(pss[:, 128:256], C("ones"), sq[:, 128:256])
            pgc = psum_m()
            mm(pgc[:, 0:128], gbc, C("tri" + sfx))
            if h == 0:
                copy(gcc, pg[:, 0:2], eng="act")
                rel(pg)
                pza = psum_m()
                for k in range(KC):
                    mm(pza[:, 0:256], HT[:, k, tsl], W1[:, k, 768:1024], start=(k == 0), stop=(k == KC - 1))
            yield
            act(rn[:, 0:128], pss[:, 0:128], AF.Ln, bias=eps_t["l2q"], scale=1.0)
            act(rn[:, 128:256], pss[:, 128:256], AF.Ln, bias=eps_t["l2k"], scale=1.0)
            rel(pss)
            dtp = W["dtp"]
            stt(dtp, pgc[:, 0:128], gcc[:, h:h + 1], C("mneg" + sfx), ALU.subtract, ALU.add)
            yield
            act(rn, rn, AF.Exp, scale=-0.5)
            DT, EG = W["DT"], W["EG"]
            act(DT, dtp, AF.Exp)
            act(EG, pgc[:, 0:128], AF.Exp)
            rel(pgc)
            yield
            qn, kn = W["qn"], W["kn"]
            tt(qn, qr, rn[:, 0:128], ALU.mult, eng="pool")
            tt(kn, kr, rn[:, 128:256], ALU.mult, eng="dve")
            DTS = W["DTS"]
            tt(DTS, DT, C("notdiag" + sfx), ALU.mult, eng="pool")
            if mode == "p":
                kds = DT[:, 127:128]
                copy(P["sm"][:, 1:2], EG[:, 127:128], eng="pool")
            else:
                kds = P["sm"][:, 5:6]
                stt(W["junk"], DT, 1.0, C("last" + sfx), ALU.mult, ALU.mult, accum=kds)
                for s in range(4):
                    copy(P["sm"][:, 1 + s:2 + s], EG[:, 32 * s + 31:32 * s + 32], eng="pool")
            if h == 0:
                thz = W["junk"] if False else W["Ya"]
                thz2 = W["Yb"]
                act(thz, pza[:, 0:128], AF.Tanh, scale=0.5)
                act(thz2, pza[:, 128:256], AF.Tanh, scale=0.5)
            yield
            pkk = psum_m()
            mm(pkk[:, 0:128], kn, kn)
            mm(pkk[:, 128:256], kn, qn)
            ptk = psum_m()
            tr(ptk[:, 0:128], kn, ident)
            tr(ptk[:, 128:256], vr, ident)
            if h == 0:
                stt(PT_["sg"][:, 0:128], thz, 1.0, pza[:, 0:128], ALU.add, ALU.mult)
                stt(PT_["sg"][:, 128:256], thz2, 1.0, pza[:, 128:256], ALU.add, ALU.mult)
                rel(pza)
            yield
            X = W["X"]
            stt(X, pkk[:, 0:128], beta, DTS, ALU.mult, ALU.mult)
            tt(P["QKT"], pkk[:, 128:256], DT, ALU.mult)
            act(P["kd"], ptk[:, 0:128], AF.Identity, scale=kds)
            act(P["vt"], ptk[:, 128:256], AF.Identity, scale=0.5)
            rel(pkk)
            rel(ptk)
            tt(P["kgT"], kn, EG, ALU.mult, eng="pool")
            tt(P["qdT"], qn, EG, ALU.mult, eng="pool")
            yield
            Y = W["Y0"]
            tt(Y, ident, X, ALU.subtract, eng="pool")
            pxt = psum_m()
            tr(pxt[:, 0:128], X, ident)
            yield
            XT = W["XT"]
            copy(XT, pxt[:, 0:128], eng="act")
            rel(pxt)
            yield
            Pm, PTm = X, XT
            pend_prod = None
            for lv in range(1, nlev + 1):
                lastlv = (lv == nlev - 1)
                pp_ = None
                if lv <= nlev - 1:
                    pp_ = psum_m()
                    if not lastlv:
                        mm(pp_[:, 0:128], PTm, Pm)
                    mm(pp_[:, 128:256], Pm, PTm)
                py = None
                if pend_prod is not None:
                    py = psum_m()
                    mm(py[:, 0:128], pend_prod, Y)
                yield
                if py is not None:
                    final = (lv == nlev)
                    Yn = P["TT"] if final else W["Ya" if lv % 2 else "Yb"]
                    tt(Yn, Y, py[:, 0:128], ALU.add)
                    rel(py)
                    Y = Yn
                if pp_ is not None:
                    PTn = W["PT%d" % (lv % 2)]
                    copy(PTn, pp_[:, 128:256], eng="act")
                    if not lastlv:
                        Pn = W["P%d" % (lv % 2)]
                        copy(Pn, pp_[:, 0:128], eng="dve")
                        Pm = Pn
                    rel(pp_)
                    PTm = PTn
                    pend_prod = PTn
                else:
                    pend_prod = None
                yield

        def recur_tail(tg, par, pO, PT_, mode, pr):
            yat = PT_["yat"]
            R = RCT[tg % 2]
            for h in range(2):
                act(R["j2"][h], pO[h][:, 0:128], AF.Square)
            yield
            for h in range(2):
                ts(R["y1"][h], R["j2"][h], 1.0 / 128.0, ALU.mult, None, ALU.add, accum=R["ss"][:, h:h + 1])
            yield
            act(R["rstd"], R["ss"], AF.Ln, bias=eps_t["rms"], scale=1.0)
            yield
            act(R["rstd"], R["rstd"], AF.Exp, scale=-0.5)
            yield
            for h in range(2):
                stt(R["y1"][h], pO[h][:, 0:128], R["rstd"][:, h:h + 1], PT_["sg"][:, h * 128:(h + 1) * 128], ALU.mult, ALU.mult)
                rel(pO[h])
            yield
            for h in range(2):
                tt(yat[:, h * 128:(h + 1) * 128], R["y1"][h], ngb, ALU.mult, eng="pool")
            yield
            pty = psum_m()
            tr(pty[:, 0:128], yat[:, 0:128], ident)
            tr(pty[:, 128:256], yat[:, 128:256], ident)
            yield
            yT = R["yT"]
            copy(yT, pty[:, 0:256], eng="act")
            rel(pty)
            yield
            pat = R["pat"]
            for half in range(2):
                ppa = psum()
                for h in range(2):
                    mm(ppa[:, 0:512], yT[:, h * 128:(h + 1) * 128], WAP[:, h, half * 512:(half + 1) * 512],
                       start=(h == 0), stop=(h == 1))
                if mode == "p":
                    copy(pat[:, half * 512:(half + 1) * 512], ppa[:, 0:512], eng="act")
                elif pr["hp"] == 0:
                    copy(PAS[:, half * 512:(half + 1) * 512], ppa[:, 0:512], eng="act")
                else:
                    tt(PAS[:, half * 512:(half + 1) * 512], PAS[:, half * 512:(half + 1) * 512], ppa[:, 0:512], ALU.add)
                yield
            if mode == "p":
                dma(dr(pa_loc[tg * 128:(tg + 1) * 128, :], "pa_loc"), pat)

        def recur(tiles, mode, pr, s_in, s_out, last_group):
            for (tg, Ps, PT_) in tiles:
                par = tg % 2
                pO = [None, None]
                vn = RC["vn"]
                if mode == "p":
                    Sold = [(SB if par == 0 else SB2)[h][0] for h in range(2)]
                    Snew = [(SB2 if par == 0 else SB)[h][0] for h in range(2)]
                    if tg == 0:
                        for h in range(2):
                            memset(Sold[h], 0.0)
                    p1 = [psum_m(), psum_m()]
                    for h in range(2):
                        mm(p1[h][:, 0:128], Ps[h]["kgT"], Sold[h])
                    yield
                    for h in range(2):
                        tt(RC["r"][h], Ps[h]["vt"], p1[h][:, 0:128], ALU.subtract)
                        rel(p1[h])
                    yield
                    p2_ = [psum_m(), psum_m()]
                    for h in range(2):
                        mm(p2_[h][:, 0:128], Ps[h]["TT"], RC["r"][h])
                    yield
                    for h in range(2):
                        act(vn[h], p2_[h][:, 0:128], AF.Identity, scale=Ps[h]["sm"][:, 0:1])
                        rel(p2_[h])
                    yield
                    p3 = [None, None]
                    for h in range(2):
                        pO[h] = psum_m()
                        mm(pO[h][:, 0:128], Ps[h]["qdT"], Sold[h], start=True, stop=False)
                        mm(pO[h][:, 0:128], Ps[h]["QKT"], vn[h], start=False, stop=True)
                        p3[h] = psum_m()
                        mm(p3[h][:, 0:128], Ps[h]["kd"], vn[h])
                    spawn(recur_tail(tg, par, pO, PT_, mode, pr))
                    yield
                    for h in range(2):
                        stt(Snew[h], Sold[h], Ps[h]["sm"][:, 1:2], p3[h][:, 0:128], ALU.mult, ALU.add)
                        rel(p3[h])
                        if last_group and tiles[-1][0] == tg:
                            dma(dr(s_out[h, :, :], "Sp"), Snew[h])
                    yield
                else:
                    kgm, qdm, kdm = [], [], []
                    for h in range(2):
                        hd = pr["hp"] * 2 + h
                        kgm.append(SX[h][0:4]); qdm.append(SX[h][4:8]); kdm.append(SX[h][8:12])
                        for s in range(4):
                            dma(SB[h][s], dr(s_in[s, hd, :, :], "sgdn"))
                            tt(kgm[h][s], Ps[h]["kgT"], C("colmask%d" % s), ALU.mult, eng="pool")
                            tt(qdm[h][s], Ps[h]["qdT"], C("colmask%d" % s), ALU.mult, eng="pool")
                            ts(kdm[h][s], Ps[h]["kd"], C("rowmask")[:, s:s + 1], ALU.mult, eng="dve")
                    yield
                    p1 = [psum_m(), psum_m()]
                    for h in range(2):
                        for s in range(4):
                            mm(p1[h][:, 0:128], kgm[h][s], SB[h][s], start=(s == 0), stop=(s == 3))
                    yield
                    for h in range(2):
                        tt(RC["r"][h], Ps[h]["vt"], p1[h][:, 0:128], ALU.subtract)
                        rel(p1[h])
                    yield
                    p2_ = [psum_m(), psum_m()]
                    for h in range(2):
                        mm(p2_[h][:, 0:128], Ps[h]["TT"], RC["r"][h])
                    yield
                    for h in range(2):
                        act(vn[h], p2_[h][:, 0:128], AF.Identity, scale=Ps[h]["sm"][:, 0:1])
                        rel(p2_[h])
                    yield
                    for h in range(2):
                        pO[h] = psum_m()
                        for s in range(4):
                            mm(pO[h][:, 0:128], qdm[h][s], SB[h][s], start=(s == 0), stop=False)
                        mm(pO[h][:, 0:128], Ps[h]["QKT"], vn[h], start=False, stop=True)
                    yield
                    for h in range(2):
                        hd = pr["hp"] * 2 + h
                        for s in range(4):
                            p3 = psum_m()
                            mm(p3[:, 0:128], kdm[h][s], vn[h])
                            stt(SB2[h][s], SB[h][s], Ps[h]["sm"][:, 1 + s:2 + s], p3[:, 0:128], ALU.mult, ALU.add)
                            rel(p3)
                            dma(dr(s_out[s, hd, :, :], "Ss"), SB2[h][s])
                        yield
                    for _ in recur_tail(tg, par, pO, PT_, mode, pr):
                        yield

        spawned = []

        def delayed(g_, n):
            for _ in range(n):
                yield
            for _ in g_:
                yield

        def spawn(g_):
            spawned.append(g_)

        def run_rr(gens, carry=None, finish_carry=False):
            live = list(gens)
            while live:
                for g_ in list(live):
                    try:
                        next(g_)
                    except StopIteration:
                        live.remove(g_)
                for g_ in list(spawned):
                    try:
                        next(g_)
                    except StopIteration:
                        spawned.remove(g_)
                if carry is not None and carry[0] is not None:
                    try:
                        next(carry[0])
                    except StopIteration:
                        carry[0] = None

        def drain_pending():
            while pend[0] is not None or spawned:
                if pend[0] is not None:
                    try:
                        next(pend[0])
                    except StopIteration:
                        pend[0] = None
                for g_ in list(spawned):
                    try:
                        next(g_)
                    except StopIteration:
                        spawned.remove(g_)

        pend = [None]
        wave_ctr = [0]

        def do_wave(tiles_local, tg0, mode, pr, s_in, s_out, last_group, defer, gi=0, extra=()):
            wpar = wave_ctr[0] % 2
            wave_ctr[0] += 1
            while spawned:
                for g_ in list(spawned):
                    try:
                        next(g_)
                    except StopIteration:
                        spawned.remove(g_)
            gens, rec_tiles = [], []
            for i, t in enumerate(tiles_local):
                PT_ = PTL[wpar * 2 + i]
                Ps = [PCH[wpar * 4 + i * 2 + h] for h in range(2)]
                for h in range(2):
                    gens.append(delayed(chain(t, h, WSL[i * 2 + h], Ps[h], PT_, mode, gi), h))
                rec_tiles.append((tg0 + i, Ps, PT_))
            run_rr(gens + list(extra), carry=pend)
            if pend[0] is not None:
                for _ in pend[0]:
                    pass
            pend[0] = recur(rec_tiles, mode, pr, s_in, s_out, last_group)
            if not defer:
                drain_pending()

        for hp in range(4):
            pr = dict(load=True, w1=w1all_b[hp], w1k="w1all_bf%d" % hp, wcaT=wcaT_all[hp], alb=alb_all[hp], dtb=dtb_all[hp],
                      wa=wa_all_b[hp], wak="wa_all_bf%d" % hp, hp=hp)
            run_rr([group_front(xs, 1, "s", pr, True, True, scaT, cas, hp == 0)])
            do_wave([0], 0, "s", pr, sgdn, Ss, True, defer=False)

        cast_weights(["w2r", "wb", "wout", "wpg", "wple"])
        NG1 = T // (128 * NT1)
        assert NT1 <= 2

        def p_front(g):
            pr_ = dict(load=(g == 0), w1=w1own_b, w1k="w1own_bf", wcaT=wcaT_own, alb=alb_own[0], dtb=dtb_own[0],
                       wa=wa_own_b, wak="wa_own_bf", hp=0)
            return group_front(xp[g * 128 * NT1:(g + 1) * 128 * NT1, :], NT1, "p", pr_, g == 0, g == NG1 - 1, None, cap, True, gi=g)

        prp = dict(hp=0)
        run_rr([p_front(0)])
        for g in range(NG1):
            extra = [p_front(g + 1)] if g + 1 < NG1 else []
            do_wave(list(range(NT1)), g * NT1, "p", prp, None, Sp, g == NG1 - 1, defer=True, gi=g, extra=extra)
        drain_pending()

        S.barrier()
        S.add("pool", lambda e: e.collective_compute("ReduceScatter", ALU.add, replica_groups=[[0, 1, 2, 3], [4, 5, 6, 7]],
                                                      ins=[pa_loc], outs=[pa_red]), r=[], w=["dram_pa_red"], dkey="cc")

        psm_state["mode"] = "rr8"
        arena_reset()
        ring.clear()
        NT = NT2
        N = 128 * NT
        XH2 = alloc(NT * 1024, shape=[NT, 1024], name="XH2")
        HT2 = alloc(KC * N // 2, shape=[KC, N], dt=BF16, name="HT2")
        YB = alloc(KC * N // 2, shape=[KC, N], dt=BF16, name="YB")
        PAT = alloc(KC * N, shape=[KC, N], name="PAT")
        GA = alloc(KC * N // 2, shape=[KC, N], dt=BF16, name="GA")
        GB = alloc(KC * N // 2, shape=[KC, N], dt=BF16, name="GB")
        WS = [alloc(KC * 768 // 2, shape=[KC * 768], dt=BF16, name="ws%d" % i) for i in range(2)]
        past = alloc(NT * 1024, shape=[NT, 1024], name="past")
        pst = alloc(NT * 256, shape=[NT, 256], name="pst")
        pT = alloc(N, shape=[2, N], dt=BF16, name="pT")
        scr2 = alloc(32, name="lnscr2")
        cup = [alloc(N + 8, name="cup%d" % i) for i in range(2)]
        carryb = alloc(16, shape=[8, 2], name="carryb")
        wsi = {"i": 0}

        def wslot():
            w = WS[wsi["i"] % 2]
            wsi["i"] += 1
            return w

        def dense_group(xsrc, psrc, NTg, mode, pa_src, ydst, first, last, halo_only=False, cb_out=None):
            Ng = 128 * NTg
            nseq = 1 if mode == "p" else 4
            L = Ng if mode == "p" else 32
            dma(XH2[:, 0:NTg, :], dr(xsrc.rearrange("(t p) d -> p t d", p=128), "x2"))
            if not halo_only:
                dma(pst[:, 0:NTg, :], dr(psrc.rearrange("(t p) d -> p t d", p=128), "p2"))
                if pa_src is not None:
                    dma(past[:, 0:NTg, :], dr(pa_src.rearrange("(t p) d -> p t d", p=128), "pa_red"))
            for t in range(NTg):
                layer_norm_tile(XH2[:, t, :], lnb[0], lnb[1], scr2)
                transpose_to_fm(XH2[:, t, :], HT2, t, KC)
            if not halo_only:
                for t in range(NTg):
                    transpose_to_fm(past[:, t, :] if pa_src is not None else PAS, PAT, t, KC)
                    transpose_to_fm(pst[:, t, :], pT, t, 2)
            for c in range(KC):
                w = wslot()
                wv = Tl(w.ap[:, 0:KC * 768].rearrange("p (k n) -> p k n", k=KC), w.k, None, True)
                dma(wv, dr(w2r_b[:, c * 768:(c + 1) * 768].rearrange("(k p) n -> p k n", p=128), "w2r_bf"))

                def proj(ty):
                    pz = psum()
                    for k in range(KC):
                        mm(pz[:, 0:Ng], wv[:, k, ty * 128:(ty + 1) * 128], HT2[:, k, 0:Ng], start=(k == 0), stop=(k == KC - 1), r32=True)
                    return pz
                pcb = proj(1)
                cbs_ = rt("cb_sb", N)
                copy(cbs_[:, 0:Ng], pcb[:, 0:Ng], eng="act")
                pub = proj(2)
                cu = cup[c % 2]
                cuv = Tl(cu.ap[:, 0:nseq * (L + 2)].rearrange("p (s l) -> p s l", s=nseq), cu.k)
                tt(cuv[:, :, 2:2 + L], Tl(cbs_.ap[:, 0:Ng].rearrange("p (s l) -> p s l", s=nseq), cbs_.k),
                   Tl(pub.ap[:, 0:Ng].rearrange("p (s l) -> p s l", s=nseq), pub.k, pub.x), ALU.mult)
                if halo_only:
                    ts(carryb[:, c, :], cuv[:, 0, L:L + 2], hon[:, 0:1], ALU.mult)
                    continue
                if mode == "p":
                    copy(cuv[:, 0, 0:2], carryb[:, c, :], eng="pool")
                    copy(carryb[:, c, :], cuv[:, 0, L:L + 2], eng="pool")
                    if last:
                        dma(dr(cb_out[c * 128:(c + 1) * 128, 0, :], "cbp"), carryb[:, c, :])
                else:
                    dma(cuv[:, :, 0:2], dr(scbT[c * 128:(c + 1) * 128, :, :], "scbT"))
                    st2 = rt("st2", 8, shape=[4, 2])
                    copy(st2, cuv[:, :, L:L + 2], eng="pool")
                    dma(dr(cb_out[c * 128:(c + 1) * 128, :, :], "cbs"), st2)
                cvb = rt("cvb", N)
                cvv = Tl(cvb.ap[:, 0:Ng].rearrange("p (s l) -> p s l", s=nseq), cvb.k)
                ts(cvv, cuv[:, :, 0:L], wcb[:, c, 0:1], ALU.mult, eng="dve")
                stt(cvv, cuv[:, :, 1:1 + L], wcb[:, c, 1:2], cvv, ALU.mult, ALU.add)
                stt(cvv, cuv[:, :, 2:2 + L], wcb[:, c, 2:3], cvv, ALU.mult, ALU.add, eng="dve")
                pzb = proj(3)
                thb = rt("thb", N)
                act(thb[:, 0:Ng], pzb[:, 0:Ng], AF.Tanh, scale=0.5)
                szb = rt("szb", N)
                stt(szb[:, 0:Ng], thb[:, 0:Ng], 1.0, pzb[:, 0:Ng], ALU.add, ALU.mult)
                tt(szb[:, 0:Ng], szb[:, 0:Ng], cvb[:, 0:Ng], ALU.mult, eng="dve")
                pbb = proj(0)
                stt(YB[:, c, 0:Ng], pbb[:, 0:Ng], 0.5, szb[:, 0:Ng], ALU.mult, ALU.mult)
                for ty, G in ((4, GA), (5, GB)):
                    pgt = proj(ty)
                    act(G[:, c, 0:Ng], pgt[:, 0:Ng], AF.Tanh, scale=0.5)
            if halo_only:
                return
            for c in range(KC):
                w = wslot()
                wv = Tl(w.ap[:, 0:KC * 128].rearrange("p (k n) -> p k n", k=KC), w.k, None, True)
                dma(wv, dr(wb_b[:, c * 128:(c + 1) * 128].rearrange("(k p) n -> p k n", p=128), "wb_bf"))
                pb_ = psum()
                for k in range(KC):
                    mm(pb_[:, 0:Ng], wv[:, k, :], YB[:, k, 0:Ng], start=(k == 0), stop=(k == KC - 1), r32=True)
                m1 = rt("m1", N)
                stt(m1[:, 0:Ng], GA[:, c, 0:Ng], 1.0, PAT[:, c, 0:Ng], ALU.add, ALU.mult)
                m2 = rt("m2", N)
                stt(m2[:, 0:Ng], GB[:, c, 0:Ng], 1.0, pb_[:, 0:Ng], ALU.add, ALU.mult)
                tt(HT2[:, c, 0:Ng], m1[:, 0:Ng], m2[:, 0:Ng], ALU.add, eng="dve")
            for half in range(2):
                w = wslot()
                wv = Tl(w.ap[:, 0:KC * 512].rearrange("p (k n) -> p k n", k=KC), w.k, None, True)
                dma(wv, dr(wout_b[:, half * 512:(half + 1) * 512].rearrange("(k p) n -> p k n", p=128), "wout_bf"))
                for t in range(NTg):
                    po = psum()
                    for k in range(KC):
                        mm(po[:, 0:512], HT2[:, k, t * 128:(t + 1) * 128], wv[:, k, :], start=(k == 0), stop=(k == KC - 1), r32=True)
                    xh = XH2[:, t, half * 512:(half + 1) * 512]
                    stt(xh, xh, 2.0 * ALPHA, po[:, 0:512], ALU.mult, ALU.add)
            for t in range(NTg):
                layer_norm_tile(XH2[:, t, :], lnb[2], lnb[3], scr2, eps_key="ln4")
                transpose_to_fm(XH2[:, t, :], YB, t, KC)
            for half in range(2):
                w = wslot()
                wv = Tl(w.ap[:, 0:KC * 512].rearrange("p (k n) -> p k n", k=KC), w.k, None, True)
                dma(wv, dr(wpg_b[:, half * 512:(half + 1) * 512].rearrange("(k p) n -> p k n", p=128), "wpg_bf"))
                w2_ = wslot()
                wv2 = Tl(w2_.ap[:, 0:2 * 512].rearrange("p (k n) -> p k n", k=2), w2_.k, None, True)
                dma(wv2, dr(wple_b[:, half * 512:(half + 1) * 512].rearrange("(k p) n -> p k n", p=128), "wple_bf"))
                for t in range(NTg):
                    pg_ = psum()
                    for k in range(KC):
                        mm(pg_[:, 0:512], YB[:, k, t * 128:(t + 1) * 128], wv[:, k, :], start=(k == 0), stop=(k == KC - 1), r32=True)
                    sgm = rt("sgm", 512)
                    act(sgm, pg_[:, 0:512], AF.Tanh, scale=0.5)
                    pp2 = psum()
                    for k in range(2):
                        mm(pp2[:, 0:512], pT[:, k, t * 128:(t + 1) * 128], wv2[:, k, :], start=(k == 0), stop=(k == 1), r32=True)
                    stt(sgm, sgm, 1.0, pp2[:, 0:512], ALU.add, ALU.mult)
                    xh = XH2[:, t, half * 512:(half + 1) * 512]
                    stt(xh, xh, 2.0 * ALPHA, sgm, ALU.mult, ALU.add)
            for t in range(NTg):
                layer_norm_tile(XH2[:, t, :], lnb[4], lnb[5], scr2, eps_key="ln4")
            dma(dr(ydst.rearrange("(t p) d -> p t d", p=128), "y"), XH2[:, 0:NTg, :])

        dense_group(xs, ps, 1, "s", None, ys, True, True, cb_out=cbs)
        dense_group(xhalo, None, 1, "p", None, None, True, False, halo_only=True)
        NG2 = T4 // N
        for g in range(NG2):
            r0 = g * N
            dense_group(x2[r0:r0 + N, :], p2[r0:r0 + N, :], NT, "p", pa_red[r0:r0 + N, :], y2[r0:r0 + N, :],
                        g == 0, g == NG2 - 1, cb_out=cbp)
        S.barrier()
        S.add("sp", lambda e: e.nop(), r=[], w=[])

        S.emit(nc, es)
    return nc


_CACHE = {}


def _pair_cols(hp):
    h0, h1 = 2 * hp, 2 * hp + 1
    cols = []
    for base in (0, 1024, 2048, OFF_ZA):
        for h in (h0, h1):
            cols.extend(range(base + h * 128, base + (h + 1) * 128))
    cols += [OFF_BETA + h0, OFF_BETA + h1, OFF_DEC + h0, OFF_DEC + h1]
    return np.asarray(cols, dtype=np.int64)


def kernel(x_prompt, x_sample, state_conv_a, state_gdn, state_conv_b, p_prompt, p_sample,
           ln_in_g, ln_in_b, w_in, w_conv_a, a_log, dt_bias, norm_a_g, w_conv_b, w_proj_a,
           w_proj_b, w_out, ln1_g, ln1_b, w_ple, w_ple_gate, ln2_g, ln2_b, _NT1=2, _NT2=4):
    f = lambda a: np.ascontiguousarray(np.asarray(a, dtype=np.float32))
    x_prompt, x_sample = f(x_prompt), f(x_sample)
    B, T, _ = x_prompt.shape
    T4 = T // 4
    key = (T, _NT1, _NT2)
    if key not in _CACHE:
        _CACHE[key] = build(T, _NT1, _NT2)
    nc = _CACHE[key]
    cnp, _ = consts_np()
    w_in0 = f(w_in)[0]
    wca0 = f(w_conv_a)[0]
    cols = [_pair_cols(hp) for hp in range(4)]
    w1all = f(np.stack([w_in0[:, c] for c in cols]))
    wcaT_all = f(np.stack([wca0[:, c[:768]].T for c in cols]))
    alb_all = f(np.stack([f(a_log)[0][[2 * hp, 2 * hp + 1]] for hp in range(4)]))
    dtb_all = f(np.stack([f(dt_bias)[0][[2 * hp, 2 * hp + 1]] for hp in range(4)]))
    wa0 = f(w_proj_a)[0]
    wa_all = f(np.stack([wa0[hp * 256:(hp + 1) * 256] for hp in range(4)]))
    w2r = f(w_in0[:, OFF_BB:].reshape(D, 6, 8, 128).transpose(0, 2, 1, 3).reshape(D, 6144))
    lnp = f(np.stack([f(ln_in_g), f(ln_in_b), f(ln1_g)[0], f(ln1_b)[0], f(ln2_g)[0], f(ln2_b)[0]]))
    sca = f(state_conv_a)[0]
    sgd = f(state_gdn)[0]
    scb = f(state_conv_b)[0]
    pp = f(p_prompt)[0]
    psm = f(p_sample)[0]
    shared = dict(w1all=w1all, wcaT_all=wcaT_all, alb_all=alb_all, dtb_all=dtb_all, normg=f(norm_a_g), w2r=w2r,
                  wb=f(w_proj_b)[0], wa_all=wa_all, wout=f(w_out)[0], wpg=f(w_ple_gate)[0], wple=f(w_ple)[0],
                  wcbT=f(f(w_conv_b)[0].T), lnp=lnp, consts=cnp)
    in_maps = []
    for c in range(8):
        b, j = c // 4, c % 4
        m = dict(shared)
        m["xp"] = x_prompt[b]
        m["x2"] = f(x_prompt[b, j * T4:(j + 1) * T4])
        m["xhalo"] = f(x_prompt[b, j * T4 - 128:j * T4]) if j > 0 else np.zeros((128, D), np.float32)
        m["halo_on"] = np.full((128, 1), 1.0 if j > 0 else 0.0, np.float32)
        m["p2"] = f(pp[b, j * T4:(j + 1) * T4])
        m["xs"] = f(x_sample[4 * c:4 * c + 4].reshape(128, D))
        m["ps"] = f(psm[4 * c:4 * c + 4].reshape(128, 256))
        st = sca[4 * c:4 * c + 4]
        m["scaT"] = f(np.stack([st[:, :, cc[:768]].transpose(2, 0, 1) for cc in cols]))
        m["sgdn"] = f(sgd[4 * c:4 * c + 4])
        m["scbT"] = f(scb[4 * c:4 * c + 4].transpose(2, 0, 1))
        m["w1own"] = w1all[j]
        m["wcaT_own"] = wcaT_all[j]
        m["alb_own"] = f(alb_all[j:j + 1])
        m["dtb_own"] = f(dtb_all[j:j + 1])
        m["wa_own"] = wa_all[j]
        in_maps.append(m)
    res = run_bass_kernel_spmd(nc, in_maps, core_ids=list(range(8)))
    R = res.results
    DS = x_sample.shape[0]
    y_p = np.zeros((B, T, D), np.float32)
    y_s = np.zeros((DS, 32, D), np.float32)
    ca_p = np.zeros((1, B, 3, QKV), np.float32)
    s_p = np.zeros((1, B, NH, 128, 128), np.float32)
    cb_p = np.zeros((1, B, 2, D), np.float32)
    ca_s = np.zeros((1, DS, 3, QKV), np.float32)
    s_s = np.zeros((1, DS, NH, 128, 128), np.float32)
    cb_s = np.zeros((1, DS, 2, D), np.float32)
    for c in range(8):
        b, j = c // 4, c % 4
        r = {k: np.asarray(v) for k, v in R[c].items()}
        y_p[b, j * T4:(j + 1) * T4] = r["y2"]
        ca_p[0, b][:, cols[j][:768]] = r["cap"][:, 0, :].T
        s_p[0, b, 2 * j:2 * j + 2] = r["Sp"]
        if j == 3:
            cb_p[0, b] = r["cbp"][:, 0, :].T
        y_s[4 * c:4 * c + 4] = r["ys"].reshape(4, 32, D)
        for hp in range(4):
            for s in range(4):
                ca_s[0, 4 * c + s][:, cols[hp][:768]] = r["cas"][hp, :, s, :].T
        s_s[0, 4 * c:4 * c + 4] = r["Ss"]
        cb_s[0, 4 * c:4 * c + 4] = r["cbs"].transpose(1, 2, 0)
    return (y_p, y_s, ca_p, s_p, cb_p, ca_s, s_s, cb_s)
```

```python
import numpy as np
from contextlib import ExitStack
import concourse.bass as bass
import concourse.mybir as mybir
from concourse.bass_utils import run_bass_kernel_spmd

F32 = mybir.dt.float32
BF16 = mybir.dt.bfloat16
F32R = mybir.dt.float32r
AF = mybir.ActivationFunctionType
ALU = mybir.AluOpType

D = 1024
KC = 8
NH = 8
QKV = 3072
OFF_ZA = 3072
OFF_BETA = 4096
OFF_DEC = 4104
OFF_BB = 4112
W1C = 1028
ALPHA = 2.0 ** 0.25
LN_EPS = 1e-5
RMS_EPS = 1e-6
L2_EPS = 1e-6
NEG = -30000.0
USE_R32 = False


class Op:
    __slots__ = ("eng", "fn", "deps", "inc", "val", "dkey", "sem", "waits", "clock")

    def __init__(self, eng, fn, dkey):
        self.eng = eng
        self.fn = fn
        self.deps = set()
        self.inc = False
        self.val = 0
        self.dkey = dkey
        self.sem = None


class Sched:
    ENGS = ("pe", "act", "dve", "pool", "sp")

    def __init__(self):
        self.q = {e: [] for e in self.ENGS}
        self.lastw = {}
        self.readers = {}
        self.fence = {e: set() for e in self.ENGS}
        self.dmas = []
        self.all_dmas = []
        self.order = []

    def add(self, eng, fn, r=(), w=(), x=(), dkey=None):
        op = Op(eng, fn, dkey)
        deps = set()
        for k in r:
            o = self.lastw.get(k)
            if o is not None:
                deps.add(o)
        for k in w:
            o = self.lastw.get(k)
            if o is not None:
                deps.add(o)
            deps.update(self.readers.get(k, ()))
        for k in x:
            o = self.lastw.get(k)
            if o is not None and (o.eng != eng or o.dkey is not None):
                deps.add(o)
        for k in r:
            self.readers.setdefault(k, []).append(op)
        for k in w:
            self.lastw[k] = op
            self.readers[k] = []
        for k in x:
            self.lastw[k] = op
            self.readers[k] = []
        deps.update(self.fence[eng])
        self.fence[eng] = set()
        deps.discard(op)
        if eng == "pe":
            deps = {d for d in deps if not (d.eng == "pe" and d.dkey is None)}
        op.deps = deps
        for d in deps:
            d.inc = True
        self.q[eng].append(op)
        self.order.append(op)
        if dkey is not None:
            op.inc = True
            self.dmas.append(op)
            self.all_dmas.append(op)
        return op

    def barrier(self):
        f = set(self.dmas)
        for e in self.ENGS:
            if self.q[e]:
                f.add(self.q[e][-1])
        for e in self.ENGS:
            self.fence[e] = set(f)
        self.dmas = []

    def emit(self, nc, es):
        sem_eng = {e: es.enter_context(nc.semaphore("s_" + e)) for e in self.ENGS}
        dsem = {}
        dcnt = {}
        for e in self.ENGS:
            c = 0
            for op in self.q[e]:
                if op.dkey is not None:
                    if op.dkey not in dsem:
                        dsem[op.dkey] = es.enter_context(nc.semaphore("d%d" % len(dsem)))
                        dcnt[op.dkey] = 0
                    inc = 1 if op.dkey == "cc" else 16
                    dcnt[op.dkey] += inc
                    op.sem = dsem[op.dkey]
                    op.val = dcnt[op.dkey]
                else:
                    op.sem = sem_eng[e]
                    if op.inc:
                        c += 1
                    op.val = c if op.inc else None
        assert len(dsem) < 200, len(dsem)
        eng_clock = {e: {} for e in self.ENGS}
        for op in self.order:
            ec = eng_clock[op.eng]
            waits = []
            for d in sorted(op.deps, key=lambda d: -len(d.clock)):
                assert d.val is not None
                if ec.get(d.sem, 0) >= d.val:
                    continue
                waits.append((d.sem, d.val))
                for k_, v_ in d.clock.items():
                    if ec.get(k_, 0) < v_:
                        ec[k_] = v_
                ec[d.sem] = d.val
            op.waits = waits
            ck = dict(ec)
            if op.val is not None:
                ck[op.sem] = op.val
            op.clock = ck
        self.n_waits = sum(len(o.waits) for o in self.order)
        self.n_deps = sum(len(o.deps) for o in self.order)
        block = es.enter_context(nc.Block())

        def run(e):
            def body(eng):
                for op in self.q[e]:
                    for s_, v_ in op.waits:
                        eng.wait_ge(s_, v_)
                    ins = op.fn(eng)
                    if op.dkey is not None:
                        if op.dkey == "cc":
                            ins.then_inc(op.sem)
                        else:
                            ins.then_inc(op.sem, 16)
                    elif op.inc:
                        ins.then_inc(op.sem, 1)
            return body

        block.tensor(run("pe"))
        block.scalar(run("act"))
        block.vector(run("dve"))
        block.gpsimd(run("pool"))
        block.sync(run("sp"))


class Tl:
    __slots__ = ("ap", "k", "x", "r", "slot")

    def __init__(self, ap, k, x=None, r=False):
        self.ap = ap
        self.k = k
        self.x = x
        self.r = r

    def __getitem__(self, idx):
        return Tl(self.ap[idx], self.k, self.x, self.r)

    @property
    def o(self):
        return self.ap.bitcast(F32R) if (self.r and USE_R32) else self.ap


def consts_np():
    c = {}
    I = np.eye(128, dtype=np.float32)
    c["ident"] = I
    c["ones"] = np.ones((128, 128), np.float32)
    c["ones128"] = np.full((128, 128), 128.0, np.float32)
    idx = np.arange(128)
    for name, blk in (("p", 128), ("s", 32)):
        same = (idx[:, None] // blk) == (idx[None, :] // blk)
        upper = (idx[:, None] <= idx[None, :]) & same
        c["tri_" + name] = upper.astype(np.float32)
        c["mneg_" + name] = np.where(upper, 0.0, NEG).astype(np.float32)
        c["notdiag_" + name] = (1.0 - I).astype(np.float32)
        last = (idx[None, :] == (idx[:, None] // blk) * blk + blk - 1)
        c["last_" + name] = last.astype(np.float32)
    for s in range(4):
        m = np.zeros((128, 128), np.float32)
        m[:, 32 * s:32 * s + 32] = 1.0
        c["colmask%d" % s] = m
    rm = np.zeros((128, 128), np.float32)
    for s in range(4):
        rm[32 * s:32 * s + 32, s] = 1.0
    c["rowmask"] = rm
    names = list(c.keys())
    arr = np.concatenate([c[n] for n in names], axis=1)
    offs = {n: i * 128 for i, n in enumerate(names)}
    return np.ascontiguousarray(arr), offs


def build(T, NT1=2, NT2=4):
    T4 = T // 4
    assert T % (128 * NT1) == 0 and T4 % (128 * NT2) == 0
    nc = bass.Bass("TRN2", target_bir_lowering=False)
    cnp, coff = consts_np()
    NCON = cnp.shape[1]

    def din(name, shape, dt=F32):
        return nc.dram_tensor(name, list(shape), dt, kind="ExternalInput").ap()

    def dout(name, shape, dt=F32):
        return nc.dram_tensor(name, list(shape), dt, kind="ExternalOutput").ap()

    xp = din("xp", [T, D])
    x2 = din("x2", [T4, D])
    xhalo = din("xhalo", [128, D])
    halo_on = din("halo_on", [128, 1])
    p2 = din("p2", [T4, 256])
    xs = din("xs", [128, D])
    ps = din("ps", [128, 256])
    scaT = din("scaT", [4, 768, 4, 3])
    sgdn = din("sgdn", [4, 8, 128, 128])
    scbT = din("scbT", [D, 4, 2])
    w1all = din("w1all", [4, D, W1C])
    w1own = din("w1own", [D, W1C])
    wcaT_all = din("wcaT_all", [4, 768, 4])
    wcaT_own = din("wcaT_own", [768, 4])
    alb_all = din("alb_all", [4, 2])
    alb_own = din("alb_own", [1, 2])
    dtb_all = din("dtb_all", [4, 2])
    dtb_own = din("dtb_own", [1, 2])
    normg = din("normg", [1, 128])
    w2r = din("w2r", [D, 8 * 768])
    wb = din("wb", [D, D])
    wa_all = din("wa_all", [4, 256, D])
    wa_own = din("wa_own", [256, D])
    wout = din("wout", [D, D])
    wpg = din("wpg", [D, D])
    wple = din("wple", [256, D])
    wcbT = din("wcbT", [D, 3])
    lnp = din("lnp", [6, D])
    cdr = din("consts", [128, NCON])

    y2 = dout("y2", [T4, D])
    cap = dout("cap", [768, 1, 3])
    Sp = dout("Sp", [2, 128, 128])
    cbp = dout("cbp", [D, 1, 2])
    ys = dout("ys", [128, D])
    cas = dout("cas", [4, 768, 4, 3])
    Ss = dout("Ss", [4, 8, 128, 128])
    cbs = dout("cbs", [D, 4, 2])

    pa_loc = nc.dram_tensor("pa_loc", [T, D], F32).ap()
    bfw = {}
    for nm, src in (("w1all", w1all), ("w1own", w1own), ("wa_all", wa_all), ("wa_own", wa_own), ("w2r", w2r),
                    ("wb", wb), ("wout", wout), ("wpg", wpg), ("wple", wple)):
        bfw[nm] = (nc.dram_tensor(nm + "_bf", list(src.shape), BF16).ap(), src)
    pa_red = nc.dram_tensor("pa_red", [T4, D], F32).ap()

    S = Sched()
    es = ExitStack()
    with es:
        PERS = 6 * 1024 + NCON + 1400
        pers = es.enter_context(nc.sbuf_tensor("pers", [128, PERS], F32))
        ARENA = 53000 - PERS
        arena = es.enter_context(nc.sbuf_tensor("arena", [128, ARENA], F32))
        banks = [es.enter_context(nc.psum_tensor("pb%d" % i, [128, 512], F32)) for i in range(8)]
        pstate = {"i": 0, "off": 0, "poff": 0, "n": 0}

        psm_state = {"mode": "rr8", "free": []}

        def psum(dt=F32):
            if psm_state["mode"] == "rr8":
                b = pstate["i"] % 8
            else:
                b = 7
            pstate["i"] += 1
            ap = banks[b][:, :]
            if dt != F32:
                ap = ap.bitcast(dt)
            return Tl(ap, "ps%d" % b, "ps%d" % b)

        def psum_managed_init():
            psm_state["mode"] = "managed"
            psm_state["free"] = [(b, hf) for b in range(7) for hf in range(2)]

        def psum_m():
            assert psm_state["free"], "out of managed PSUM slots"
            b, hf = psm_state["free"].pop(0)
            t = Tl(banks[b][:, hf * 256:(hf + 1) * 256], "ps%d" % b, "ps%d" % b)
            t.slot = (b, hf)
            return t

        def rel(t):
            psm_state["free"].append(t.slot)

        def alloc(words, shape=None, dt=F32, pool="arena", name=None, r32=False):
            key = "off" if pool == "arena" else "poff"
            base = arena if pool == "arena" else pers
            lim = ARENA if pool == "arena" else PERS
            o = pstate[key]
            assert o + words <= lim, (pool, o, words, lim)
            pstate[key] = o + words
            ap = base[:, o:o + words]
            if dt != F32:
                ap = ap.bitcast(dt)
            if shape is not None:
                names = " ".join("a%d" % i for i in range(len(shape)))
                kw = {"a%d" % i: s for i, s in enumerate(shape)}
                ap = ap.rearrange("p (%s) -> p %s" % (names, names), **kw)
            pstate["n"] += 1
            return Tl(ap, name or ("t%d" % pstate["n"]), None, r32)

        def arena_reset():
            pstate["off"] = 0

        def keys(ts):
            ks, xs_ = [], []
            for t in ts:
                if t is None:
                    continue
                if t.x is not None:
                    xs_.append(t.x)
                else:
                    ks.append(t.k)
            return ks, xs_

        def emit(eng, fn, r, w):
            rk, rx = keys(r)
            wk, wx = keys(w)
            return S.add(eng, fn, r=rk, w=wk, x=rx + wx)

        def dma(out, in_, eng="sp"):
            rk, _ = keys([in_])
            wk, _ = keys([out])
            dkey = "dma_" + out.k if not out.k.startswith("dram") else "dma_" + in_.k
            return S.add(eng, lambda e: e.dma_start(out=out.o, in_=(in_.ap.bitcast(F32R) if (out.r and USE_R32) else in_.ap)), r=rk, w=wk, dkey=dkey)

        def act(out, in_, func, bias=None, scale=None, accum=None, eng="act"):
            r = [in_]
            kw = {}
            if isinstance(bias, Tl):
                r.append(bias); kw["bias"] = bias.ap
            elif bias is not None:
                kw["bias"] = bias
            if isinstance(scale, Tl):
                r.append(scale); kw["scale"] = scale.ap
            elif scale is not None:
                kw["scale"] = scale
            w = [out]
            if accum is not None:
                w.append(accum); kw["accum_out"] = accum.ap
            return emit(eng, lambda e: e.activation(out=out.o, in_=in_.ap, func=func, **kw), r, w)

        def tt(out, a, b, op, eng="dve"):
            return emit(eng, lambda e: e.tensor_tensor(out=out.o, in0=a.ap, in1=b.ap, op=op), [a, b], [out])

        def ts(out, a, s1, op0, s2=None, op1=None, eng="dve", accum=None):
            r = [a]
            v1 = s1.ap if isinstance(s1, Tl) else s1
            v2 = s2.ap if isinstance(s2, Tl) else s2
            if isinstance(s1, Tl):
                r.append(s1)
            if isinstance(s2, Tl):
                r.append(s2)
            kw = {}
            w = [out]
            if op1 is not None:
                kw["op1"] = op1
            if accum is not None:
                kw["accum_out"] = accum.ap
                w.append(accum)
            return emit(eng, lambda e: e.tensor_scalar(out=out.o, in0=a.ap, scalar1=v1, scalar2=v2, op0=op0, **kw), r, w)

        def stt(out, a, s, b, op0, op1, eng="dve", accum=None):
            r = [a, b]
            v = s.ap if isinstance(s, Tl) else s
            if isinstance(s, Tl):
                r.append(s)
            kw = {}
            w = [out]
            if accum is not None:
                kw["accum_out"] = accum.ap
                w.append(accum)
            return emit(eng, lambda e: e.scalar_tensor_tensor(out=out.o, in0=a.ap, scalar=v, in1=b.ap, op0=op0, op1=op1, **kw), r, w)

        def copy(out, in_, eng="act"):
            if eng == "act":
                return act(out, in_, AF.Copy)
            return emit(eng, lambda e: e.tensor_copy(out=out.o, in_=in_.ap), [in_], [out])

        def mm(out, lhsT, rhs, start=True, stop=True, r32=False):
            la, ra = lhsT.ap, rhs.ap
            if r32 and USE_R32:
                la = la.bitcast(F32R)
                ra = ra.bitcast(F32R)
            return emit("pe", lambda e: e.matmul(out.ap, la, ra, start=start, stop=stop), [lhsT, rhs], [out])

        def tr(out, in_, ident):
            return emit("pe", lambda e: e.transpose(out.ap, in_.ap, ident.ap), [in_, ident], [out])

        def memset(out, val, eng="pool"):
            return emit(eng, lambda e: e.memset(out.ap, val), [], [out])

        def dr(ap, name):
            return Tl(ap, "dram_" + name)

        def cast_weights(names):
            for nm in names:
                dst, src = bfw[nm]
                if len(dst.shape) == 3:
                    for i in range(dst.shape[0]):
                        dma(dr(dst[i], "%s_bf%d" % (nm, i)), dr(src[i], "%s_src%d" % (nm, i)), eng="pool")
                else:
                    rows = dst.shape[0]
                    step = 512
                    for r0 in range(0, rows, step):
                        r1 = min(rows, r0 + step)
                        dma(dr(dst[r0:r1, :], "%s_bf_r%d" % (nm, r0) if rows > step and nm not in ("w1own", "wa_own") else nm + "_bf"),
                            dr(src[r0:r1, :], "%s_src%d" % (nm, r0)), eng="pool")
        cast_weights(["w1all", "wa_all", "w1own", "wa_own"])
        w1all_b, w1own_b, wa_all_b, wa_own_b = bfw["w1all"][0], bfw["w1own"][0], bfw["wa_all"][0], bfw["wa_own"][0]
        w2r_b, wb_b, wout_b, wpg_b, wple_b = bfw["w2r"][0], bfw["wb"][0], bfw["wout"][0], bfw["wpg"][0], bfw["wple"][0]

        lnb = [alloc(1024, pool="pers", name="lnb%d" % i) for i in range(6)]
        cst = alloc(NCON, pool="pers", name="consts")
        for i in range(6):
            dma(lnb[i], dr(lnp[i, :].partition_broadcast(128), "lnp"))
        dma(cst, dr(cdr, "consts"))

        def C(name):
            o = coff[name]
            return cst[:, o:o + 128]

        ident = C("ident")
        PAS = alloc(1024, pool="pers", name="PAS")
        ngb = alloc(128, pool="pers", name="ngb")
        dma(ngb, dr(normg[0, :].partition_broadcast(128), "normg"))
        ts(ngb, ngb, 0.5, ALU.mult)
        wcb = alloc(24, shape=[8, 3], pool="pers", name="wcb")
        dma(wcb, dr(wcbT.rearrange("(c p) j -> p c j", p=128), "wcbT"))
        hon = alloc(1, pool="pers", name="hon")
        dma(hon, dr(halo_on, "halo_on"))
        eps_t = {}
        for nm, v in (("ln", LN_EPS), ("one", 1.0), ("l2q", 4 * L2_EPS * 128.0), ("l2k", 4 * L2_EPS), ("rms", RMS_EPS), ("ln4", 4 * LN_EPS)):
            t = alloc(1, pool="pers", name="c_" + nm)
            memset(t, v)
            eps_t[nm] = t

        def layer_norm_tile(xt, g_bc, b_bc, scr, eps_key="ln"):
            st = scr[:, 0:12]
            mv = scr[:, 12:14]
            rs = scr[:, 14:15]
            nb = scr[:, 15:16]
            emit("dve", lambda e: e.bn_stats(out=st.ap[:, 0:6], in_=xt.ap[:, 0:512]), [xt], [st])
            emit("dve", lambda e: e.bn_stats(out=st.ap[:, 6:12], in_=xt.ap[:, 512:1024]), [xt], [st])
            emit("dve", lambda e: e.bn_aggr(out=mv.ap, in_=st.ap), [st], [mv])
            act(rs, mv[:, 1:2], AF.Ln, bias=eps_t[eps_key], scale=1.0)
            act(rs, rs, AF.Exp, scale=-0.5)
            stt(nb, mv[:, 0:1], -1.0, rs, ALU.mult, ALU.mult)
            act(xt, xt, AF.Identity, bias=nb, scale=rs)
            tt(xt, xt, g_bc, ALU.mult, eng="dve")
            tt(xt, xt, b_bc, ALU.add, eng="dve")

        def transpose_to_fm(src, dstT, t, ncols_chunks):
            for k0 in range(0, ncols_chunks, 4):
                n = min(4, ncols_chunks - k0)
                pt = psum()
                for k in range(n):
                    tr(pt[:, k * 128:(k + 1) * 128], src[:, (k0 + k) * 128:(k0 + k + 1) * 128], ident)
                o = dstT[:, k0:k0 + n, t * 128:(t + 1) * 128]
                i = Tl(pt.ap[:, 0:n * 128].rearrange("p (k n) -> p k n", k=n), pt.k, pt.x)
                copy(o, i, eng="act")

        psum_managed_init()
        arena_reset()
        NTm = NT1
        Nm = 128 * NTm
        XH = alloc(NTm * 1024, shape=[NTm, 1024], name="XH")
        HTs = [alloc(KC * Nm // 2, shape=[KC, Nm], dt=BF16, name="HT%d" % i) for i in range(2)]
        W1 = alloc(KC * W1C // 2, shape=[KC, W1C], dt=BF16, name="W1")
        wca = alloc(24, shape=[6, 4], name="wca")
        dtbb = alloc(2, name="dtbb")
        nAb = alloc(2, name="nAb")
        scr = alloc(32, name="lnscr")
        ZP = [alloc(Nm + 3, name="zpad%d" % i) for i in range(2)]
        carry = alloc(18, shape=[6, 3], name="carry")
        FMs = [[alloc(Nm, name="fm%d_%d" % (i, c)) for c in range(6)] for i in range(2)]
        cvt = [alloc(Nm, name="cvt%d" % i) for i in range(2)]
        SMs = [alloc(NTm * 4, shape=[NTm, 4], name="smalltok%d" % i) for i in range(2)]
        SB = [[alloc(128, name="S_%d_%d" % (h, s)) for s in range(4)] for h in range(2)]
        SB2 = [[alloc(128, name="S2_%d_%d" % (h, s)) for s in range(4)] for h in range(2)]
        WAP = alloc(1024, shape=[2, 1024], dt=BF16, name="WAP")
        NRING = 2
        ring = {}

        def rt(name, words=128, shape=None, dt=F32, r32=False):
            if name not in ring:
                ring[name] = [[alloc(words, shape=shape, dt=dt, name="%s_%d" % (name, i), r32=r32) for i in range(NRING)], 0]
            lst = ring[name]
            t = lst[0][lst[1] % NRING]
            lst[1] += 1
            return t

        WNAMES = ["qn", "kn", "gbc", "dtp", "DT", "EG", "DTS", "X", "XT", "Y0", "P0", "P1", "PT0", "PT1", "Ya", "Yb", "junk"]
        WSL = []
        for i in range(4):
            d_ = {n: alloc(128, name="w%d_%s" % (i, n)) for n in WNAMES}
            d_["sq"] = alloc(256, name="w%d_sq" % i)
            d_["rn"] = alloc(256, name="w%d_rn" % i)
            WSL.append(d_)
        PCH = []
        for i in range(8):
            d_ = {n: alloc(128, name="p%d_%s" % (i, n)) for n in ("TT", "QKT", "kgT", "qdT", "kd", "vt")}
            d_["sm"] = alloc(8, name="p%d_sm" % i)
            PCH.append(d_)
        PTL = []
        for i in range(4):
            PTL.append(dict(sg=alloc(256, name="pt%d_sg" % i), yat=alloc(256, name="pt%d_yat" % i), gcc=alloc(2, name="pt%d_gcc" % i)))
        RC = dict(r=[alloc(128, name="rc_r%d" % h) for h in range(2)], vn=[alloc(128, name="rc_vn%d" % h) for h in range(2)])
        RCT = [dict(y1=[alloc(128, name="rt%d_y1%d" % (i, h)) for h in range(2)], j2=[alloc(128, name="rt%d_j2%d" % (i, h)) for h in range(2)],
                    ss=alloc(2, name="rt%d_ss" % i), rstd=alloc(2, name="rt%d_rstd" % i),
                    yT=alloc(128, shape=[256], dt=BF16, name="rt%d_yT" % i), pat=alloc(1024, name="rt%d_pat" % i)) for i in range(2)]
        SX = []
        for i in range(2):
            base_names = ["qn", "kn", "gbc", "dtp", "DT", "EG", "DTS", "X", "XT", "Y0", "P0", "P1"]
            SX.append([WSL[2 + i][n] for n in base_names])

        def group_front(xsrc, NT, mode, pr, first_group, last_group, halo_src, ca_out, prep, gi=0):
            HT, FM, SM = HTs[gi % 2], FMs[gi % 2], SMs[gi % 2]
            N = 128 * NT
            nseq = 1 if mode == "p" else 4
            L = N if mode == "p" else 32
            if prep:
                dma(XH[:, 0:NT, :], dr(xsrc.rearrange("(t p) d -> p t d", p=128), "x"))
                for t in range(NT):
                    layer_norm_tile(XH[:, t, :], lnb[0], lnb[1], scr)
                    yield
                    transpose_to_fm(XH[:, t, :], HT, t, KC)
                    yield
            if pr.get("load"):
                dma(W1, dr(pr["w1"].rearrange("(k p) n -> p k n", p=128), pr["w1k"]))
                dma(wca, dr(pr["wcaT"].rearrange("(c p) j -> p c j", p=128), "wcaT"))
                dma(dtbb, dr(pr["dtb"].partition_broadcast(128), "dtb"))
                dma(nAb, dr(pr["alb"].partition_broadcast(128), "alb"))
                dma(WAP, dr(pr["wa"].rearrange("(k p) n -> p k n", p=128), pr["wak"]))
                act(nAb, nAb, AF.Exp)
                ts(nAb, nAb, -1.0, ALU.mult)
            prev_silu = None
            for c in range(6):
                pz = psum()
                for k in range(KC):
                    mm(pz[:, 0:N], W1[:, k, c * 128:(c + 1) * 128], HT[:, k, 0:N], start=(k == 0), stop=(k == KC - 1))
                zp = ZP[c % 2]
                zv = Tl(zp.ap[:, 0:nseq * (L + 3)].rearrange("p (s l) -> p s l", s=nseq), zp.k)
                pv = Tl(pz.ap[:, 0:N].rearrange("p (s l) -> p s l", s=nseq), pz.k, pz.x)
                copy(zv[:, :, 3:3 + L], pv, eng="act")
                if mode == "p":
                    if first_group:
                        memset(zv[:, :, 0:3], 0.0)
                    else:
                        copy(zv[:, 0, 0:3], carry[:, c, :], eng="pool")
                else:
                    dma(zv[:, :, 0:3], dr(halo_src[pr["hp"], c * 128:(c + 1) * 128, :, :], "scaT"))
                if mode == "p":
                    copy(carry[:, c, :], zv[:, 0, L:L + 3], eng="pool")
                    if last_group:
                        dma(dr(ca_out[c * 128:(c + 1) * 128, 0, :], "cap"), carry[:, c, :])
                else:
                    st3 = rt("st3", 12, shape=[4, 3])
                    copy(st3, zv[:, :, L:L + 3], eng="pool")
                    dma(dr(ca_out[pr["hp"], c * 128:(c + 1) * 128, :, :], "cas"), st3)
                yield
                cv = cvt[c % 2]
                cvv = Tl(cv.ap[:, 0:N].rearrange("p (s l) -> p s l", s=nseq), cv.k)
                ts(cvv, zv[:, :, 0:L], wca[:, c, 0:1], ALU.mult, eng="dve")
                for j in (1, 2, 3):
                    stt(cvv, zv[:, :, j:j + L], wca[:, c, j:j + 1], cvv, ALU.mult, ALU.add, eng="dve")
                yield
                if prev_silu is not None:
                    for _ in prev_silu():
                        yield

                def silu_steps(c=c, cv=cv):
                    th = rt("tanh_fm", Nm)
                    act(th[:, 0:N], cv[:, 0:N], AF.Tanh, scale=0.5)
                    yield
                    stt(FM[c][:, 0:N], th[:, 0:N], 1.0, cv[:, 0:N], ALU.add, ALU.mult)
                    yield
                prev_silu = silu_steps
            for _ in prev_silu():
                yield
            psm = psum()
            for t in range(NT):
                for k in range(KC):
                    mm(psm[:, t * 4:(t + 1) * 4], HT[:, k, t * 128:(t + 1) * 128], W1[:, k, 1024:1028], start=(k == 0), stop=(k == KC - 1))
            psv = Tl(psm.ap[:, 0:NT * 4].rearrange("p (t f) -> p t f", t=NT), psm.k, psm.x)
            act(SM[:, 0:NT, 0:2], psv[:, :, 0:2], AF.Tanh, scale=0.5)
            for t in range(NT):
                tt(SM[:, t, 2:4], psv[:, t, 2:4], dtbb, ALU.add)
            yield
            ts(SM[:, 0:NT, 0:2], SM[:, 0:NT, 0:2], 1.0, ALU.add, 0.5, ALU.mult)
            act(SM[:, 0:NT, 2:4], SM[:, 0:NT, 2:4], AF.Exp)
            yield
            act(SM[:, 0:NT, 2:4], SM[:, 0:NT, 2:4], AF.Ln, bias=eps_t["one"], scale=1.0)
            yield
            for t in range(NT):
                tt(SM[:, t, 2:4], SM[:, t, 2:4], nAb, ALU.mult)
            yield

        def chain(t, h, W, P, PT_, mode, gi=0):
            HT, FM, SM = HTs[gi % 2], FMs[gi % 2], SMs[gi % 2]
            sfx = "_p" if mode == "p" else "_s"
            nlev = 7 if mode == "p" else 5
            tsl = slice(t * 128, (t + 1) * 128)
            gcc = PT_["gcc"]
            beta = P["sm"][:, 0:1]
            qr, kr, vr = FM[h][:, tsl], FM[2 + h][:, tsl], FM[4 + h][:, tsl]
            sq, rn = W["sq"], W["rn"]
            copy(beta, SM[:, t, h:h + 1], eng="pool")
            act(sq[:, 0:128], qr, AF.Square)
            act(sq[:, 128:256], kr, AF.Square)
            gbc = W["gbc"]
            ts(gbc, C("ones"), SM[:, t, 2 + h:3 + h], ALU.mult, eng="pool")
            if h == 0:
                pg = psum_m()
                mm(pg[:, 0:2], C("tri" + sfx), SM[:, t, 2:4])
            yield
            pss = psum_m()
            mm(pss[:, 0:128], C("ones128"), sq[:, 0:128])
            mm(pss[:, 128:256], C("ones"), sq[:, 128:256])
            pgc = psum_m()
            mm(pgc[:, 0:128], gbc, C("tri" + sfx))
            if h == 0:
                copy(gcc, pg[:, 0:2], eng="act")
                rel(pg)
                pza = psum_m()
                for k in range(KC):
                    mm(pza[:, 0:256], HT[:, k, tsl], W1[:, k, 768:1024], start=(k == 0), stop=(k == KC - 1))
            yield
            act(rn[:, 0:128], pss[:, 0:128], AF.Ln, bias=eps_t["l2q"], scale=1.0)
            act(rn[:, 128:256], pss[:, 128:256], AF.Ln, bias=eps_t["l2k"], scale=1.0)
            rel(pss)
            dtp = W["dtp"]
            stt(dtp, pgc[:, 0:128], gcc[:, h:h + 1], C("mneg" + sfx), ALU.subtract, ALU.add)
            yield
            act(rn, rn, AF.Exp, scale=-0.5)
            DT, EG = W["DT"], W["EG"]
            act(DT, dtp, AF.Exp)
            act(EG, pgc[:, 0:128], AF.Exp)
            rel(pgc)
            yield
            qn, kn = W["qn"], W["kn"]
            tt(qn, qr, rn[:, 0:128], ALU.mult, eng="pool")
            tt(kn, kr, rn[:, 128:256], ALU.mult, eng="dve")
            DTS = W["DTS"]
            tt(DTS, DT, C("notdiag" + sfx), ALU.mult, eng="pool")
            if mode == "p":
                kds = DT[:, 127:128]
                copy(P["sm"][:, 1:2], EG[:, 127:128], eng="pool")
            else:
                kds = P["sm"][:, 5:6]
                stt(W["junk"], DT, 1.0, C("last" + sfx), ALU.mult, ALU.mult, accum=kds)
                for s in range(4):
                    copy(P["sm"][:, 1 + s:2 + s], EG[:, 32 * s + 31:32 * s + 32], eng="pool")
            if h == 0:
                thz = W["junk"] if False else W["Ya"]
                thz2 = W["Yb"]
                act(thz, pza[:, 0:128], AF.Tanh, scale=0.5)
                act(thz2, pza[:, 128:256], AF.Tanh, scale=0.5)
            yield
            pkk = psum_m()
            mm(pkk[:, 0:128], kn, kn)
            mm(pkk[:, 128:256], kn, qn)
            ptk = psum_m()
            tr(ptk[:, 0:128], kn, ident)
            tr(ptk[:, 128:256], vr, ident)
            if h == 0:
                stt(PT_["sg"][:, 0:128], thz, 1.0, pza[:, 0:128], ALU.add, ALU.mult)
                stt(PT_["sg"][:, 128:256], thz2, 1.0, pza[:, 128:256], ALU.add, ALU.mult)
                rel(pza)
            yield
            X = W["X"]
            stt(X, pkk[:, 0:128], beta, DTS, ALU.mult, ALU.mult)
            tt(P["QKT"], pkk[:, 128:256], DT, ALU.mult)
            act(P["kd"], ptk[:, 0:128], AF.Identity, scale=kds)
            act(P["vt"], ptk[:, 128:256], AF.Identity, scale=0.5)
            rel(pkk)
            rel(ptk)
            tt(P["kgT"], kn, EG, ALU.mult, eng="pool")
            tt(P["qdT"], qn, EG, ALU.mult, eng="pool")
            yield
            Y = W["Y0"]
            tt(Y, ident, X, ALU.subtract, eng="pool")
            pxt = psum_m()
            tr(pxt[:, 0:128], X, ident)
            yield
            XT = W["XT"]
            copy(XT, pxt[:, 0:128], eng="act")
            rel(pxt)
            yield
            Pm, PTm = X, XT
            pend_prod = None
            for lv in range(1, nlev + 1):
                lastlv = (lv == nlev - 1)
                pp_ = None
                if lv <= nlev - 1:
                    pp_ = psum_m()
                    if not lastlv:
                        mm(pp_[:, 0:128], PTm, Pm)
                    mm(pp_[:, 128:256], Pm, PTm)
                py = None
                if pend_prod is not None:
                    py = psum_m()
                    mm(py[:, 0:128], pend_prod, Y)
                yield
                if py is not None:
                    final = (lv == nlev)
                    Yn = P["TT"] if final else W["Ya" if lv % 2 else "Yb"]
                    tt(Yn, Y, py[:, 0:128], ALU.add)
                    rel(py)
                    Y = Yn
                if pp_ is not None:
                    PTn = W["PT%d" % (lv % 2)]
                    copy(PTn, pp_[:, 128:256], eng="act")
                    if not lastlv:
                        Pn = W["P%d" % (lv % 2)]
                        copy(Pn, pp_[:, 0:128], eng="dve")
                        Pm = Pn
                    rel(pp_)
                    PTm = PTn
                    pend_prod = PTn
                else:
                    pend_prod = None
                yield

        def recur_tail(tg, par, pO, PT_, mode, pr):
            yat = PT_["yat"]
            R = RCT[tg % 2]
            for h in range(2):
                act(R["j2"][h], pO[h][:, 0:128], AF.Square)
            yield
            for h in range(2):
                ts(R["y1"][h], R["j2"][h], 1.0 / 128.0, ALU.mult, None, ALU.add, accum=R["ss"][:, h:h + 1])
            yield
            act(R["rstd"], R["ss"], AF.Ln, bias=eps_t["rms"], scale=1.0)
            yield
            act(R["rstd"], R["rstd"], AF.Exp, scale=-0.5)
            yield
            for h in range(2):
                stt(R["y1"][h], pO[h][:, 0:128], R["rstd"][:, h:h + 1], PT_["sg"][:, h * 128:(h + 1) * 128], ALU.mult, ALU.mult)
                rel(pO[h])
            yield
            for h in range(2):
                tt(yat[:, h * 128:(h + 1) * 128], R["y1"][h], ngb, ALU.mult, eng="pool")
            yield
            pty = psum_m()
            tr(pty[:, 0:128], yat[:, 0:128], ident)
            tr(pty[:, 128:256], yat[:, 128:256], ident)
            yield
            yT = R["yT"]
            copy(yT, pty[:, 0:256], eng="act")
            rel(pty)
            yield
            pat = R["pat"]
            for half in range(2):
                ppa = psum()
                for h in range(2):
                    mm(ppa[:, 0:512], yT[:, h * 128:(h + 1) * 128], WAP[:, h, half * 512:(half + 1) * 512],
                       start=(h == 0), stop=(h == 1))
                if mode == "p":
                    copy(pat[:, half * 512:(half + 1) * 512], ppa[:, 0:512], eng="act")
                elif pr["hp"] == 0:
                    copy(PAS[:, half * 512:(half + 1) * 512], ppa[:, 0:512], eng="act")
                else:
                    tt(PAS[:, half * 512:(half + 1) * 512], PAS[:, half * 512:(half + 1) * 512], ppa[:, 0:512], ALU.add)
                yield
            if mode == "p":
                dma(dr(pa_loc[tg * 128:(tg + 1) * 128, :], "pa_loc"), pat)

        def recur(tiles, mode, pr, s_in, s_out, last_group):
            for (tg, Ps, PT_) in tiles:
                par = tg % 2
                pO = [None, None]
                vn = RC["vn"]
                if mode == "p":
                    Sold = [(SB if par == 0 else SB2)[h][0] for h in range(2)]
                    Snew = [(SB2 if par == 0 else SB)[h][0] for h in range(2)]
                    if tg == 0:
                        for h in range(2):
                            memset(Sold[h], 0.0)
                    p1 = [psum_m(), psum_m()]
                    for h in range(2):
                        mm(p1[h][:, 0:128], Ps[h]["kgT"], Sold[h])
                    yield
                    for h in range(2):
                        tt(RC["r"][h], Ps[h]["vt"], p1[h][:, 0:128], ALU.subtract)
                        rel(p1[h])
                    yield
                    p2_ = [psum_m(), psum_m()]
                    for h in range(2):
                        mm(p2_[h][:, 0:128], Ps[h]["TT"], RC["r"][h])
                    yield
                    for h in range(2):
                        act(vn[h], p2_[h][:, 0:128], AF.Identity, scale=Ps[h]["sm"][:, 0:1])
                        rel(p2_[h])
                    yield
                    p3 = [None, None]
                    for h in range(2):
                        pO[h] = psum_m()
                        mm(pO[h][:, 0:128], Ps[h]["qdT"], Sold[h], start=True, stop=False)
                        mm(pO[h][:, 0:128], Ps[h]["QKT"], vn[h], start=False, stop=True)
                        p3[h] = psum_m()
                        mm(p3[h][:, 0:128], Ps[h]["kd"], vn[h])
                    spawn(recur_tail(tg, par, pO, PT_, mode, pr))
                    yield
                    for h in range(2):
                        stt(Snew[h], Sold[h], Ps[h]["sm"][:, 1:2], p3[h][:, 0:128], ALU.mult, ALU.add)
                        rel(p3[h])
                        if last_group and tiles[-1][0] == tg:
                            dma(dr(s_out[h, :, :], "Sp"), Snew[h])
                    yield
                else:
                    kgm, qdm, kdm = [], [], []
                    for h in range(2):
                        hd = pr["hp"] * 2 + h
                        kgm.append(SX[h][0:4]); qdm.append(SX[h][4:8]); kdm.append(SX[h][8:12])
                        for s in range(4):
                            dma(SB[h][s], dr(s_in[s, hd, :, :], "sgdn"))
                            tt(kgm[h][s], Ps[h]["kgT"], C("colmask%d" % s), ALU.mult, eng="pool")
                            tt(qdm[h][s], Ps[h]["qdT"], C("colmask%d" % s), ALU.mult, eng="pool")
                            ts(kdm[h][s], Ps[h]["kd"], C("rowmask")[:, s:s + 1], ALU.mult, eng="dve")
                    yield
                    p1 = [psum_m(), psum_m()]
                    for h in range(2):
                        for s in range(4):
                            mm(p1[h][:, 0:128], kgm[h][s], SB[h][s], start=(s == 0), stop=(s == 3))
                    yield
                    for h in range(2):
                        tt(RC["r"][h], Ps[h]["vt"], p1[h][:, 0:128], ALU.subtract)
                        rel(p1[h])
                    yield
                    p2_ = [psum_m(), psum_m()]
                    for h in range(2):
                        mm(p2_[h][:, 0:128], Ps[h]["TT"], RC["r"][h])
                    yield
                    for h in range(2):
                        act(vn[h], p2_[h][:, 0:128], AF.Identity, scale=Ps[h]["sm"][:, 0:1])
                        rel(p2_[h])
                    yield
                    for h in range(2):
                        pO[h] = psum_m()
                        for s in range(4):
                            mm(pO[h][:, 0:128], qdm[h][s], SB[h][s], start=(s == 0), stop=False)
                        mm(pO[h][:, 0:128], Ps[h]["QKT"], vn[h], start=False, stop=True)
                    yield
                    for h in range(2):
                        hd = pr["hp"] * 2 + h
                        for s in range(4):
                            p3 = psum_m()
                            mm(p3[:, 0:128], kdm[h][s], vn[h])
                            stt(SB2[h][s], SB[h][s], Ps[h]["sm"][:, 1 + s:2 + s], p3[:, 0:128], ALU.mult, ALU.add)
                            rel(p3)
                            dma(dr(s_out[s, hd, :, :], "Ss"), SB2[h][s])
                        yield
                    for _ in recur_tail(tg, par, pO, PT_, mode, pr):
                        yield

        spawned = []

        def delayed(g_, n):
            for _ in range(n):
                yield
            for _ in g_:
                yield

        def spawn(g_):
            spawned.append(g_)

        def run_rr(gens, carry=None, finish_carry=False):
            live = list(gens)
            while live:
                for g_ in list(live):
                    try:
                        next(g_)
                    except StopIteration:
                        live.remove(g_)
                for g_ in list(spawned):
                    try:
                        next(g_)
                    except StopIteration:
                        spawned.remove(g_)
                if carry is not None and carry[0] is not None:
                    try:
                        next(carry[0])
                    except StopIteration:
                        carry[0] = None

        def drain_pending():
            while pend[0] is not None or spawned:
                if pend[0] is not None:
                    try:
                        next(pend[0])
                    except StopIteration:
                        pend[0] = None
                for g_ in list(spawned):
                    try:
                        next(g_)
                    except StopIteration:
                        spawned.remove(g_)

        pend = [None]
        wave_ctr = [0]

        def do_wave(tiles_local, tg0, mode, pr, s_in, s_out, last_group, defer, gi=0, extra=()):
            wpar = wave_ctr[0] % 2
            wave_ctr[0] += 1
            while spawned:
                for g_ in list(spawned):
                    try:
                        next(g_)
                    except StopIteration:
                        spawned.remove(g_)
            gens, rec_tiles = [], []
            for i, t in enumerate(tiles_local):
                PT_ = PTL[wpar * 2 + i]
                Ps = [PCH[wpar * 4 + i * 2 + h] for h in range(2)]
                for h in range(2):
                    gens.append(delayed(chain(t, h, WSL[i * 2 + h], Ps[h], PT_, mode, gi), 0))
                rec_tiles.append((tg0 + i, Ps, PT_))
            run_rr(gens + list(extra), carry=pend)
            if pend[0] is not None:
                for _ in pend[0]:
                    pass
            pend[0] = recur(rec_tiles, mode, pr, s_in, s_out, last_group)
            if not defer:
                drain_pending()

        for hp in range(4):
            pr = dict(load=True, w1=w1all_b[hp], w1k="w1all_bf%d" % hp, wcaT=wcaT_all[hp], alb=alb_all[hp], dtb=dtb_all[hp],
                      wa=wa_all_b[hp], wak="wa_all_bf%d" % hp, hp=hp)
            run_rr([group_front(xs, 1, "s", pr, True, True, scaT, cas, hp == 0)])
            do_wave([0], 0, "s", pr, sgdn, Ss, True, defer=False)

        cast_weights(["w2r", "wb", "wout", "wpg", "wple"])
        NG1 = T // (128 * NT1)
        assert NT1 <= 2

        def p_front(g):
            pr_ = dict(load=(g == 0), w1=w1own_b, w1k="w1own_bf", wcaT=wcaT_own, alb=alb_own[0], dtb=dtb_own[0],
                       wa=wa_own_b, wak="wa_own_bf", hp=0)
            return group_front(xp[g * 128 * NT1:(g + 1) * 128 * NT1, :], NT1, "p", pr_, g == 0, g == NG1 - 1, None, cap, True, gi=g)

        prp = dict(hp=0)
        run_rr([p_front(0)])
        for g in range(NG1):
            extra = [p_front(g + 1)] if g + 1 < NG1 else []
            do_wave(list(range(NT1)), g * NT1, "p", prp, None, Sp, g == NG1 - 1, defer=True, gi=g, extra=extra)
        drain_pending()

        S.barrier()
        S.add("pool", lambda e: e.collective_compute("ReduceScatter", ALU.add, replica_groups=[[0, 1, 2, 3], [4, 5, 6, 7]],
                                                      ins=[pa_loc], outs=[pa_red]), r=[], w=["dram_pa_red"], dkey="cc")

        psm_state["mode"] = "rr8"
        arena_reset()
        ring.clear()
        NT = NT2
        N = 128 * NT
        XH2 = alloc(NT * 1024, shape=[NT, 1024], name="XH2")
        HT2 = alloc(KC * N // 2, shape=[KC, N], dt=BF16, name="HT2")
        YB = alloc(KC * N // 2, shape=[KC, N], dt=BF16, name="YB")
        PAT = alloc(KC * N, shape=[KC, N], name="PAT")
        GA = alloc(KC * N // 2, shape=[KC, N], dt=BF16, name="GA")
        GB = alloc(KC * N // 2, shape=[KC, N], dt=BF16, name="GB")
        WS = [alloc(KC * 768 // 2, shape=[KC * 768], dt=BF16, name="ws%d" % i) for i in range(2)]
        past = alloc(NT * 1024, shape=[NT, 1024], name="past")
        pst = alloc(NT * 256, shape=[NT, 256], name="pst")
        pT = alloc(N, shape=[2, N], dt=BF16, name="pT")
        scr2 = alloc(32, name="lnscr2")
        cup = [alloc(N + 8, name="cup%d" % i) for i in range(2)]
        carryb = alloc(16, shape=[8, 2], name="carryb")
        wsi = {"i": 0}

        def wslot():
            w = WS[wsi["i"] % 2]
            wsi["i"] += 1
            return w

        def dense_group(xsrc, psrc, NTg, mode, pa_src, ydst, first, last, halo_only=False, cb_out=None):
            Ng = 128 * NTg
            nseq = 1 if mode == "p" else 4
            L = Ng if mode == "p" else 32
            dma(XH2[:, 0:NTg, :], dr(xsrc.rearrange("(t p) d -> p t d", p=128), "x2"))
            if not halo_only:
                dma(pst[:, 0:NTg, :], dr(psrc.rearrange("(t p) d -> p t d", p=128), "p2"))
                if pa_src is not None:
                    dma(past[:, 0:NTg, :], dr(pa_src.rearrange("(t p) d -> p t d", p=128), "pa_red"))
            for t in range(NTg):
                layer_norm_tile(XH2[:, t, :], lnb[0], lnb[1], scr2)
                transpose_to_fm(XH2[:, t, :], HT2, t, KC)
            if not halo_only:
                for t in range(NTg):
                    transpose_to_fm(past[:, t, :] if pa_src is not None else PAS, PAT, t, KC)
                    transpose_to_fm(pst[:, t, :], pT, t, 2)
            for c in range(KC):
                w = wslot()
                wv = Tl(w.ap[:, 0:KC * 768].rearrange("p (k n) -> p k n", k=KC), w.k, None, True)
                dma(wv, dr(w2r_b[:, c * 768:(c + 1) * 768].rearrange("(k p) n -> p k n", p=128), "w2r_bf"))

                def proj(ty):
                    pz = psum()
                    for k in range(KC):
                        mm(pz[:, 0:Ng], wv[:, k, ty * 128:(ty + 1) * 128], HT2[:, k, 0:Ng], start=(k == 0), stop=(k == KC - 1), r32=True)
                    return pz
                pcb = proj(1)
                cbs_ = rt("cb_sb", N)
                copy(cbs_[:, 0:Ng], pcb[:, 0:Ng], eng="act")
                pub = proj(2)
                cu = cup[c % 2]
                cuv = Tl(cu.ap[:, 0:nseq * (L + 2)].rearrange("p (s l) -> p s l", s=nseq), cu.k)
                tt(cuv[:, :, 2:2 + L], Tl(cbs_.ap[:, 0:Ng].rearrange("p (s l) -> p s l", s=nseq), cbs_.k),
                   Tl(pub.ap[:, 0:Ng].rearrange("p (s l) -> p s l", s=nseq), pub.k, pub.x), ALU.mult)
                if halo_only:
                    ts(carryb[:, c, :], cuv[:, 0, L:L + 2], hon[:, 0:1], ALU.mult)
                    continue
                if mode == "p":
                    copy(cuv[:, 0, 0:2], carryb[:, c, :], eng="pool")
                    copy(carryb[:, c, :], cuv[:, 0, L:L + 2], eng="pool")
                    if last:
                        dma(dr(cb_out[c * 128:(c + 1) * 128, 0, :], "cbp"), carryb[:, c, :])
                else:
                    dma(cuv[:, :, 0:2], dr(scbT[c * 128:(c + 1) * 128, :, :], "scbT"))
                    st2 = rt("st2", 8, shape=[4, 2])
                    copy(st2, cuv[:, :, L:L + 2], eng="pool")
                    dma(dr(cb_out[c * 128:(c + 1) * 128, :, :], "cbs"), st2)
                cvb = rt("cvb", N)
                cvv = Tl(cvb.ap[:, 0:Ng].rearrange("p (s l) -> p s l", s=nseq), cvb.k)
                ts(cvv, cuv[:, :, 0:L], wcb[:, c, 0:1], ALU.mult, eng="dve")
                stt(cvv, cuv[:, :, 1:1 + L], wcb[:, c, 1:2], cvv, ALU.mult, ALU.add)
                stt(cvv, cuv[:, :, 2:2 + L], wcb[:, c, 2:3], cvv, ALU.mult, ALU.add, eng="dve")
                pzb = proj(3)
                thb = rt("thb", N)
                act(thb[:, 0:Ng], pzb[:, 0:Ng], AF.Tanh, scale=0.5)
                szb = rt("szb", N)
                stt(szb[:, 0:Ng], thb[:, 0:Ng], 1.0, pzb[:, 0:Ng], ALU.add, ALU.mult)
                tt(szb[:, 0:Ng], szb[:, 0:Ng], cvb[:, 0:Ng], ALU.mult, eng="dve")
                pbb = proj(0)
                stt(YB[:, c, 0:Ng], pbb[:, 0:Ng], 0.5, szb[:, 0:Ng], ALU.mult, ALU.mult)
                for ty, G in ((4, GA), (5, GB)):
                    pgt = proj(ty)
                    act(G[:, c, 0:Ng], pgt[:, 0:Ng], AF.Tanh, scale=0.5)
            if halo_only:
                return
            for c in range(KC):
                w = wslot()
                wv = Tl(w.ap[:, 0:KC * 128].rearrange("p (k n) -> p k n", k=KC), w.k, None, True)
                dma(wv, dr(wb_b[:, c * 128:(c + 1) * 128].rearrange("(k p) n -> p k n", p=128), "wb_bf"))
                pb_ = psum()
                for k in range(KC):
                    mm(pb_[:, 0:Ng], wv[:, k, :], YB[:, k, 0:Ng], start=(k == 0), stop=(k == KC - 1), r32=True)
                m1 = rt("m1", N)
                stt(m1[:, 0:Ng], GA[:, c, 0:Ng], 1.0, PAT[:, c, 0:Ng], ALU.add, ALU.mult)
                m2 = rt("m2", N)
                stt(m2[:, 0:Ng], GB[:, c, 0:Ng], 1.0, pb_[:, 0:Ng], ALU.add, ALU.mult)
                tt(HT2[:, c, 0:Ng], m1[:, 0:Ng], m2[:, 0:Ng], ALU.add, eng="dve")
            for half in range(2):
                w = wslot()
                wv = Tl(w.ap[:, 0:KC * 512].rearrange("p (k n) -> p k n", k=KC), w.k, None, True)
                dma(wv, dr(wout_b[:, half * 512:(half + 1) * 512].rearrange("(k p) n -> p k n", p=128), "wout_bf"))
                for t in range(NTg):
                    po = psum()
                    for k in range(KC):
                        mm(po[:, 0:512], HT2[:, k, t * 128:(t + 1) * 128], wv[:, k, :], start=(k == 0), stop=(k == KC - 1), r32=True)
                    xh = XH2[:, t, half * 512:(half + 1) * 512]
                    stt(xh, xh, 2.0 * ALPHA, po[:, 0:512], ALU.mult, ALU.add)
            for t in range(NTg):
                layer_norm_tile(XH2[:, t, :], lnb[2], lnb[3], scr2, eps_key="ln4")
                transpose_to_fm(XH2[:, t, :], YB, t, KC)
            for half in range(2):
                w = wslot()
                wv = Tl(w.ap[:, 0:KC * 512].rearrange("p (k n) -> p k n", k=KC), w.k, None, True)
                dma(wv, dr(wpg_b[:, half * 512:(half + 1) * 512].rearrange("(k p) n -> p k n", p=128), "wpg_bf"))
                w2_ = wslot()
                wv2 = Tl(w2_.ap[:, 0:2 * 512].rearrange("p (k n) -> p k n", k=2), w2_.k, None, True)
                dma(wv2, dr(wple_b[:, half * 512:(half + 1) * 512].rearrange("(k p) n -> p k n", p=128), "wple_bf"))
                for t in range(NTg):
                    pg_ = psum()
                    for k in range(KC):
                        mm(pg_[:, 0:512], YB[:, k, t * 128:(t + 1) * 128], wv[:, k, :], start=(k == 0), stop=(k == KC - 1), r32=True)
                    sgm = rt("sgm", 512)
                    act(sgm, pg_[:, 0:512], AF.Tanh, scale=0.5)
                    pp2 = psum()
                    for k in range(2):
                        mm(pp2[:, 0:512], pT[:, k, t * 128:(t + 1) * 128], wv2[:, k, :], start=(k == 0), stop=(k == 1), r32=True)
                    stt(sgm, sgm, 1.0, pp2[:, 0:512], ALU.add, ALU.mult)
                    xh = XH2[:, t, half * 512:(half + 1) * 512]
                    stt(xh, xh, 2.0 * ALPHA, sgm, ALU.mult, ALU.add)
            for t in range(NTg):
                layer_norm_tile(XH2[:, t, :], lnb[4], lnb[5], scr2, eps_key="ln4")
            dma(dr(ydst.rearrange("(t p) d -> p t d", p=128), "y"), XH2[:, 0:NTg, :])

        dense_group(xs, ps, 1, "s", None, ys, True, True, cb_out=cbs)
        dense_group(xhalo, None, 1, "p", None, None, True, False, halo_only=True)
        NG2 = T4 // N
        for g in range(NG2):
            r0 = g * N
            dense_group(x2[r0:r0 + N, :], p2[r0:r0 + N, :], NT, "p", pa_red[r0:r0 + N, :], y2[r0:r0 + N, :],
                        g == 0, g == NG2 - 1, cb_out=cbp)
        S.barrier()
        S.add("sp", lambda e: e.nop(), r=[], w=[])

        S.emit(nc, es)
    return nc


_CACHE = {}


def _pair_cols(hp):
    h0, h1 = 2 * hp, 2 * hp + 1
    cols = []
    for base in (0, 1024, 2048, OFF_ZA):
        for h in (h0, h1):
            cols.extend(range(base + h * 128, base + (h + 1) * 128))
    cols += [OFF_BETA + h0, OFF_BETA + h1, OFF_DEC + h0, OFF_DEC + h1]
    return np.asarray(cols, dtype=np.int64)


def kernel(x_prompt, x_sample, state_conv_a, state_gdn, state_conv_b, p_prompt, p_sample,
           ln_in_g, ln_in_b, w_in, w_conv_a, a_log, dt_bias, norm_a_g, w_conv_b, w_proj_a,
           w_proj_b, w_out, ln1_g, ln1_b, w_ple, w_ple_gate, ln2_g, ln2_b, _NT1=2, _NT2=4):
    f = lambda a: np.ascontiguousarray(np.asarray(a, dtype=np.float32))
    x_prompt, x_sample = f(x_prompt), f(x_sample)
    B, T, _ = x_prompt.shape
    T4 = T // 4
    key = (T, _NT1, _NT2)
    if key not in _CACHE:
        _CACHE[key] = build(T, _NT1, _NT2)
    nc = _CACHE[key]
    cnp, _ = consts_np()
    w_in0 = f(w_in)[0]
    wca0 = f(w_conv_a)[0]
    cols = [_pair_cols(hp) for hp in range(4)]
    w1all = f(np.stack([w_in0[:, c] for c in cols]))
    wcaT_all = f(np.stack([wca0[:, c[:768]].T for c in cols]))
    alb_all = f(np.stack([f(a_log)[0][[2 * hp, 2 * hp + 1]] for hp in range(4)]))
    dtb_all = f(np.stack([f(dt_bias)[0][[2 * hp, 2 * hp + 1]] for hp in range(4)]))
    wa0 = f(w_proj_a)[0]
    wa_all = f(np.stack([wa0[hp * 256:(hp + 1) * 256] for hp in range(4)]))
    w2r = f(w_in0[:, OFF_BB:].reshape(D, 6, 8, 128).transpose(0, 2, 1, 3).reshape(D, 6144))
    lnp = f(np.stack([f(ln_in_g), f(ln_in_b), f(ln1_g)[0], f(ln1_b)[0], f(ln2_g)[0], f(ln2_b)[0]]))
    sca = f(state_conv_a)[0]
    sgd = f(state_gdn)[0]
    scb = f(state_conv_b)[0]
    pp = f(p_prompt)[0]
    psm = f(p_sample)[0]
    shared = dict(w1all=w1all, wcaT_all=wcaT_all, alb_all=alb_all, dtb_all=dtb_all, normg=f(norm_a_g), w2r=w2r,
                  wb=f(w_proj_b)[0], wa_all=wa_all, wout=f(w_out)[0], wpg=f(w_ple_gate)[0], wple=f(w_ple)[0],
                  wcbT=f(f(w_conv_b)[0].T), lnp=lnp, consts=cnp)
    in_maps = []
    for c in range(8):
        b, j = c // 4, c % 4
        m = dict(shared)
        m["xp"] = x_prompt[b]
        m["x2"] = f(x_prompt[b, j * T4:(j + 1) * T4])
        m["xhalo"] = f(x_prompt[b, j * T4 - 128:j * T4]) if j > 0 else np.zeros((128, D), np.float32)
        m["halo_on"] = np.full((128, 1), 1.0 if j > 0 else 0.0, np.float32)
        m["p2"] = f(pp[b, j * T4:(j + 1) * T4])
        m["xs"] = f(x_sample[4 * c:4 * c + 4].reshape(128, D))
        m["ps"] = f(psm[4 * c:4 * c + 4].reshape(128, 256))
        st = sca[4 * c:4 * c + 4]
        m["scaT"] = f(np.stack([st[:, :, cc[:768]].transpose(2, 0, 1) for cc in cols]))
        m["sgdn"] = f(sgd[4 * c:4 * c + 4])
        m["scbT"] = f(scb[4 * c:4 * c + 4].transpose(2, 0, 1))
        m["w1own"] = w1all[j]
        m["wcaT_own"] = wcaT_all[j]
        m["alb_own"] = f(alb_all[j:j + 1])
        m["dtb_own"] = f(dtb_all[j:j + 1])
        m["wa_own"] = wa_all[j]
        in_maps.append(m)
    res = run_bass_kernel_spmd(nc, in_maps, core_ids=list(range(8)))
    R = res.results
    DS = x_sample.shape[0]
    y_p = np.zeros((B, T, D), np.float32)
    y_s = np.zeros((DS, 32, D), np.float32)
    ca_p = np.zeros((1, B, 3, QKV), np.float32)
    s_p = np.zeros((1, B, NH, 128, 128), np.float32)
    cb_p = np.zeros((1, B, 2, D), np.float32)
    ca_s = np.zeros((1, DS, 3, QKV), np.float32)
    s_s = np.zeros((1, DS, NH, 128, 128), np.float32)
    cb_s = np.zeros((1, DS, 2, D), np.float32)
    for c in range(8):
        b, j = c // 4, c % 4
        r = {k: np.asarray(v) for k, v in R[c].items()}
        y_p[b, j * T4:(j + 1) * T4] = r["y2"]
        ca_p[0, b][:, cols[j][:768]] = r["cap"][:, 0, :].T
        s_p[0, b, 2 * j:2 * j + 2] = r["Sp"]
        if j == 3:
            cb_p[0, b] = r["cbp"][:, 0, :].T
        y_s[4 * c:4 * c + 4] = r["ys"].reshape(4, 32, D)
        for hp in range(4):
            for s in range(4):
                ca_s[0, 4 * c + s][:, cols[hp][:768]] = r["cas"][hp, :, s, :].T
        s_s[0, 4 * c:4 * c + 4] = r["Ss"]
        cb_s[0, 4 * c:4 * c + 4] = r["cbs"].transpose(1, 2, 0)
    return (y_p, y_s, ca_p, s_p, cb_p, ca_s, s_s, cb_s)
```
